# Optimizing a Trainium2 kernel written in Bass

```python
import math
import jax, jax.numpy as jnp
from jax import lax
import numpy as np

D_MODEL = 1024
BATCH = 32
SEQ = 2048
DEPTH = 2

EXPAND = 2
BRANCH_W = EXPAND * D_MODEL
N_A = DEPTH // 2
N_B = DEPTH - N_A
CHUNK = 128
GROUP_W = 128
N_GROUPS = BRANCH_W // GROUP_W
DIFF_HEAD_DIM = 128
N_DIFF_HEADS = BRANCH_W // (2 * DIFF_HEAD_DIM)
ATT_W = N_DIFF_HEADS * 2 * DIFF_HEAD_DIM
Q_BLOCK = 128
EPS = 1e-6

kernel_name = "yoco_gmlp_diffattn_hybrid"


def rmsnorm(x, g):
    x32 = x.astype(jnp.float32)
    y = x32 * lax.rsqrt(jnp.mean(x32 * x32, axis=-1, keepdims=True) + EPS)
    return (y * g.astype(jnp.float32)).astype(x.dtype)


def layernorm(x, g, b):
    x32 = x.astype(jnp.float32)
    mu = jnp.mean(x32, axis=-1, keepdims=True)
    xc = x32 - mu
    y = xc * lax.rsqrt(jnp.mean(xc * xc, axis=-1, keepdims=True) + EPS)
    return (y * g.astype(jnp.float32) + b.astype(jnp.float32)).astype(x.dtype)


def lambda_init_fn(layer_idx):
    return 0.8 - 0.6 * math.exp(-0.3 * layer_idx)


def mixer_a(hn, w_in, ln_g, ln_b, w_s, b_s, w_out):
    bsz, seq, _ = hn.shape
    u, v, z = jnp.split(hn @ w_in, 3, axis=-1)
    u = jax.nn.gelu(u)
    v = layernorm(jax.nn.gelu(v), ln_g, ln_b)
    vc = v.reshape(bsz, seq // CHUNK, CHUNK, N_GROUPS, GROUP_W)
    causal = jnp.tril(jnp.ones((CHUNK, CHUNK), dtype=bool))
    ws = jnp.where(causal[None], w_s, jnp.zeros((), w_s.dtype))
    sv = jnp.einsum('gts,bnsgc->bntgc', ws, vc) + jnp.transpose(b_s)[None, None, :, :, None]
    y = u * sv.reshape(bsz, seq, BRANCH_W) * jax.nn.silu(z)
    return y @ w_out


def shared_kv(h, kv_norm_g, w_kv):
    bsz, seq, _ = h.shape
    k, v = jnp.split(rmsnorm(h, kv_norm_g) @ w_kv, 2, axis=-1)
    k = k.reshape(bsz, seq, N_DIFF_HEADS, 2, DIFF_HEAD_DIM)
    v = v.reshape(bsz, seq, N_DIFF_HEADS, 2 * DIFF_HEAD_DIM)
    return k, v


def diff_attention(q, k, v, lam):
    seq = q.shape[1]
    scale = DIFF_HEAD_DIM ** -0.5
    outs = []
    for i in range(seq // Q_BLOCK):
        start, end = i * Q_BLOCK, (i + 1) * Q_BLOCK
        qs = q[:, start:end]
        ks = k[:, :end]
        vs = v[:, :end]
        s = jnp.einsum('bqhnd,bkhnd->bhnqk', qs, ks).astype(jnp.float32) * scale
        q_pos = jnp.arange(start, end)
        k_pos = jnp.arange(end)
        mask = k_pos[None, :] <= q_pos[:, None]
        s = jnp.where(mask, s, -jnp.inf)
        p = jax.nn.softmax(s, axis=-1)
        w = p[:, :, 0] - lam * p[:, :, 1]
        outs.append(jnp.einsum('bhqk,bkhe->bqhe', w.astype(vs.dtype), vs))
    return jnp.concatenate(outs, axis=1)


def mixer_b(hn, k, v, w_qz, lq1, lk1, lq2, lk2, subln_g, w_o, lam_init):
    bsz, seq, _ = hn.shape
    q, z = jnp.split(hn @ w_qz, 2, axis=-1)
    q = q.reshape(bsz, seq, N_DIFF_HEADS, 2, DIFF_HEAD_DIM)
    f32 = jnp.float32
    lam = (jnp.exp(jnp.sum(lq1.astype(f32) * lk1.astype(f32)))
           - jnp.exp(jnp.sum(lq2.astype(f32) * lk2.astype(f32))) + lam_init)
    o = diff_attention(q, k, v, lam)
    o = rmsnorm(o, subln_g) * (1.0 - lam_init)
    y = o.reshape(bsz, seq, ATT_W) * jax.nn.silu(z)
    return y @ w_o


def setup_inputs(seed: int = 0) -> dict:
    key = jax.random.key(seed)
    ks = jax.random.split(key, 24)
    f32 = jnp.float32
    D, E, d = D_MODEL, BRANCH_W, DIFF_HEAD_DIM
    nrm = lambda k, shape, s: jax.random.normal(k, shape, f32) * s
    return {
        "x": jax.random.normal(ks[0], (BATCH, SEQ, D), f32),
        "a_norm_g": 1.0 + nrm(ks[1], (N_A, D), 0.02),
        "a_w_in": nrm(ks[2], (N_A, D, 3 * E), D ** -0.5),
        "a_ln_g": 1.0 + nrm(ks[3], (N_A, E), 0.02),
        "a_ln_b": nrm(ks[4], (N_A, E), 0.02),
        "a_w_s": nrm(ks[5], (N_A, N_GROUPS, CHUNK, CHUNK), CHUNK ** -0.5),
        "a_b_s": 1.0 + nrm(ks[6], (N_A, N_GROUPS, CHUNK), 0.02),
        "a_w_out": nrm(ks[7], (N_A, E, D), E ** -0.5),
        "b_norm_g": 1.0 + nrm(ks[8], (N_B, D), 0.02),
        "b_w_qz": nrm(ks[9], (N_B, D, 2 * ATT_W), D ** -0.5),
        "b_lam_q1": nrm(ks[10], (N_B, d), 0.1),
        "b_lam_k1": nrm(ks[11], (N_B, d), 0.1),
        "b_lam_q2": nrm(ks[12], (N_B, d), 0.1),
        "b_lam_k2": nrm(ks[13], (N_B, d), 0.1),
        "b_subln_g": 1.0 + nrm(ks[14], (N_B, 2 * d), 0.02),
        "b_w_o": nrm(ks[15], (N_B, ATT_W, D), ATT_W ** -0.5),
        "kv_norm_g": 1.0 + nrm(ks[16], (D,), 0.02),
        "w_kv": nrm(ks[17], (D, 2 * ATT_W), D ** -0.5),
        "final_g": 1.0 + nrm(ks[18], (D,), 0.02),
    }


def reference(x, a_norm_g, a_w_in, a_ln_g, a_ln_b, a_w_s, a_b_s, a_w_out,
              b_norm_g, b_w_qz, b_lam_q1, b_lam_k1, b_lam_q2, b_lam_k2, b_subln_g, b_w_o,
              kv_norm_g, w_kv, final_g):
    h = x
    k_sh, v_sh = None, None
    for l in range(DEPTH):
        if l < N_A:
            h = h + mixer_a(rmsnorm(h, a_norm_g[l]), a_w_in[l], a_ln_g[l], a_ln_b[l],
                            a_w_s[l], a_b_s[l], a_w_out[l])
        else:
            if l == N_A:
                k_sh, v_sh = shared_kv(h, kv_norm_g, w_kv)
            j = l - N_A
            h = h + mixer_b(rmsnorm(h, b_norm_g[j]), k_sh, v_sh, b_w_qz[j],
                            b_lam_q1[j], b_lam_k1[j], b_lam_q2[j], b_lam_k2[j],
                            b_subln_g[j], b_w_o[j], lambda_init_fn(l))
    return rmsnorm(h, final_g)
```

```python
import math
from contextlib import ExitStack
import numpy as np
import concourse.bass as bass
import concourse.mybir as mybir
from concourse.bass_utils import run_bass_kernel_spmd

F32 = mybir.dt.float32
BF16 = mybir.dt.bfloat16
AF = mybir.ActivationFunctionType
ALU = mybir.AluOpType
AX = mybir.AxisListType

D = 1024
E = 2048
EPS = 1e-6
NCORES = 8
LAM_INIT = 0.8 - 0.6 * math.exp(-0.3 * 1)
ATT_SCALE = 128 ** -0.5


class SemW:
    def __init__(self, h, name):
        self.h = h
        self.name = name
        self.count = 0


class T:
    def __init__(self, name, psum=False):
        self.name = name
        self.w = None
        self.r = {}
        self.psum = psum


class Eng:
    def __init__(self, trk, name, eng, compute=True):
        self.trk = trk
        self.name = name
        self.eng = eng
        self.compute = compute
        self.semw = None
        self.waited = {}
        self.pending = False
        if compute:
            self.new_epoch()

    def new_epoch(self):
        assert not self.pending
        self.semw = self.trk.new_sem(self.name)

    def wait(self, dep):
        semw, val = dep[0], dep[1]
        if self.waited.get(semw, 0) >= val:
            return
        self.eng.wait_ge(semw.h, val)
        self.waited[semw] = val


class Tracker:
    def __init__(self, nc, es):
        self.nc = nc
        self.es = es
        self.nsem = 0

    def new_sem(self, name):
        self.nsem += 1
        h = self.es.enter_context(self.nc.semaphore(f"s{self.nsem}_{name}"))
        return SemW(h, name)

    def _deps(self, E, reads, writes):
        deps = []
        for t in reads:
            if t.w is not None:
                deps.append((t.w, "raw"))
            if t.psum:
                for k, d in t.r.items():
                    if d[2] != E.name:
                        deps.append((d, "rar"))
        for t in writes:
            if t.w is not None:
                deps.append((t.w, "waw"))
            for d in t.r.values():
                deps.append((d, "war"))
        for d, kind in deps:
            if d[2] == E.name and d[0] is E.semw:
                if E.name == "pe" or kind != "raw":
                    continue
                if d[1] > d[0].count:
                    continue
            E.wait(d)

    def op(self, E, emit, reads=(), writes=(), mark=True):
        self._deps(E, reads, writes)
        inst = emit()
        if mark:
            E.semw.count += 1
            inst.then_inc(E.semw.h, 1)
            E.pending = False
            val = E.semw.count
        else:
            E.pending = True
            val = E.semw.count + 1
        dep = (E.semw, val, E.name)
        for t in reads:
            t.r[E.name] = dep
        for t in writes:
            t.w = dep
            t.r = {}
        return inst

    def dma(self, Q, semw, out, in_, reads=(), writes=(), **kw):
        self._deps(Q, reads, writes)
        inst = Q.eng.dma_start(out=out, in_=in_, **kw)
        inst.then_inc(semw.h, 16)
        semw.count += 16
        dep = (semw, semw.count, "dma")
        for t in reads:
            t.r["dma_" + semw.name] = dep
        for t in writes:
            t.w = dep
            t.r = {}
        return inst


def build(NSEQ=4, SEQ=2048, do_a=True, do_b=True, final_norm=True):
    nc = bass.Bass("TRN2", target_bir_lowering=False)
    NT = SEQ // 128
    NBLK = SEQ // 512
    NTOK = NSEQ * SEQ

    def din(name, shape):
        return nc.dram_tensor(name, list(shape), F32, kind="ExternalInput").ap()

    x = din("x", [NTOK, D])
    a_norm_g = din("a_norm_g", [D])
    a_w_in = din("a_w_in", [D, 3 * E])
    a_ln_g = din("a_ln_g", [E])
    a_ln_b = din("a_ln_b", [E])
    a_w_s = din("a_w_s", [16, 128, 128])
    a_b_s = din("a_b_s", [16 * 128])
    a_w_out = din("a_w_out", [E, D])
    b_norm_g = din("b_norm_g", [D])
    b_w_qz = din("b_w_qz", [D, 2 * E])
    b_lam = [din(n, [128]) for n in ("b_lam_q1", "b_lam_k1", "b_lam_q2", "b_lam_k2")]
    b_subln_g = din("b_subln_g", [256])
    b_w_o = din("b_w_o", [E, D])
    kv_norm_g = din("kv_norm_g", [D])
    w_kv = din("w_kv", [D, 2 * E])
    final_g = din("final_g", [D])
    out = nc.dram_tensor("out", [NTOK, D], F32, kind="ExternalOutput").ap()

    win_s = nc.dram_tensor("win_s", [12, 128, 4096], BF16).ap()
    wkv_s = nc.dram_tensor("wkv_s", [8, 128, 4096], BF16).ap()
    wqz_s = nc.dram_tensor("wqz_s", [8, 128, 4096], BF16).ap()
    wout_s = nc.dram_tensor("wout_s", [4, 128, 4096], BF16).ap()
    wo_s = nc.dram_tensor("wo_s", [4, 128, 4096], BF16).ap()

    es = ExitStack()
    with es:
        trk = Tracker(nc, es)
        PE = Eng(trk, "pe", nc.tensor)
        ACT = Eng(trk, "act", nc.scalar)
        DVE = Eng(trk, "dve", nc.vector)
        POOL = Eng(trk, "pool", nc.gpsimd)
        SP = Eng(trk, "sp", nc.sync, compute=False)
        engines = [PE, ACT, DVE, POOL]

        sbtot = {"b": 0}

        def sb(name, shape, dt):
            n = 1
            for d_ in shape[1:]:
                n *= d_
            sbtot["b"] += n * (4 if dt == F32 else 2)
            return es.enter_context(nc.sbuf_tensor(name, list(shape), dt))

        def barrier():
            for Ei in engines + [SP]:
                for Ej in engines:
                    if Ej is Ei or Ej.semw.count == 0:
                        continue
                    assert not Ej.pending
                    Ei.wait((Ej.semw, Ej.semw.count))

        h_sb = sb("h", [128, NT, D], F32)
        h_t = [T(f"h{i}") for i in range(NT)]
        wsl = sb("wsl", [128, 8, 2048], BF16)
        wsl_t = [T(f"wsl{i}") for i in range(8)]
        wsl_sem = [trk.new_sem(f"wsl{i}") for i in range(8)]
        arena = sb("arena", [128, 39040], BF16)
        ident = sb("ident", [128, 128], BF16)
        mask01 = sb("mask01", [128, 128], BF16)
        wsT = sb("wsT", [128, 16, 128], BF16)
        biasT = sb("biasT", [128, 16, 128], F32)
        lngT = sb("lngT", [128, 16], F32)
        lnbT = sb("lnbT", [128, 16], F32)
        gA = sb("gA", [128, 8], F32)
        gKV = sb("gKV", [128, 8], F32)
        gB = sb("gB", [128, 8], F32)
        sgb = sb("sgb", [128, 256], F32)
        fgb = sb("fgb", [128, D], F32)
        neglam = sb("neglam", [128, 1], F32)
        mhalf = sb("mhalf", [128, 16], F32)
        small = sb("small", [128, 64], F32)
        sqj = sb("sqj", [128, D], BF16)
        hs = [sb(f"hs{i}", [128, D], BF16) for i in range(2)]
        hs_t = [T(f"hs{i}") for i in range(2)]
        tmpb = [arena[:, 24576 + i * 512:24576 + (i + 1) * 512] for i in range(3)]
        tmpb_t = [T(f"tmpb{i}") for i in range(3)]
        st_sem = [trk.new_sem(f"st{i}") for i in range(4)]
        x_sem = [trk.new_sem(f"x{i}") for i in range(4)]
        c_sem = trk.new_sem("const")
        consts_t = T("consts")
        sqj_t = T("sqj")
        small_t = T("small")

        banks = [es.enter_context(nc.psum_tensor(f"bank{i}", [128, 512], F32)) for i in range(8)]
        bank_t = [T(f"bank{i}", psum=True) for i in range(8)]
        pool_state = {"ids": list(range(8)), "nxt": 0}

        def nbank():
            ids = pool_state["ids"]
            i = ids[pool_state["nxt"] % len(ids)]
            pool_state["nxt"] += 1
            return i

        rr = {"tmpb": 0, "ostg": 0, "hs": 0, "ev": 0}

        def evac_engine():
            rr["ev"] += 1
            return ACT if rr["ev"] % 2 else DVE

        def copy_on(Eg, out_ap, in_ap, reads, writes):
            if Eg is ACT:
                trk.op(ACT, lambda: nc.scalar.activation(out=out_ap, in_=in_ap, func=AF.Copy), reads, writes)
            else:
                trk.op(Eg, lambda: Eg.eng.tensor_copy(out=out_ap, in_=in_ap), reads, writes)

        lamscr = arena[:].bitcast(F32)[:, 5000:5768]
        cl_t = T("cl")
        ws_sem = trk.new_sem("wsld")

        def setup_consts():
            with nc.allow_non_contiguous_dma(reason="tiny one-time parameter loads"):
                def cl(dst, src):
                    trk.dma(SP, c_sem, dst, src, writes=[cl_t])
                cl(lngT[:], a_ln_g.rearrange("(g c) -> c g", c=128))
                cl(lnbT[:], a_ln_b.rearrange("(g c) -> c g", c=128))
                cl(gA[:], a_norm_g.rearrange("(k p) -> p k", p=128))
                cl(gKV[:], kv_norm_g.rearrange("(k p) -> p k", p=128))
                cl(gB[:], b_norm_g.rearrange("(k p) -> p k", p=128))
                cl(sgb[:], b_subln_g.partition_broadcast(128))
                cl(fgb[:], final_g.partition_broadcast(128))
                cl(biasT[:].rearrange("p g t -> p (g t)"), a_b_s.partition_broadcast(128))
                for i in range(4):
                    cl(lamscr[:, i * 128:(i + 1) * 128], b_lam[i].partition_broadcast(128))
            consts_t.w = (c_sem, c_sem.count, "dma")
            trk.op(POOL, lambda: nc.gpsimd.memset(ident[:], 0.0), writes=[consts_t])
            trk.op(POOL, lambda: nc.gpsimd.affine_select(out=ident[:], in_=ident[:], pattern=[[-1, 128]],
                                                         compare_op=ALU.not_equal, fill=1.0, base=0,
                                                         channel_multiplier=1), reads=[consts_t], writes=[consts_t])
            trk.op(POOL, lambda: nc.gpsimd.memset(mask01[:], 1.0), writes=[consts_t])
            trk.op(POOL, lambda: nc.gpsimd.affine_select(out=mask01[:], in_=mask01[:], pattern=[[1, 128]],
                                                         compare_op=ALU.is_ge, fill=0.0, base=0,
                                                         channel_multiplier=-1), reads=[consts_t], writes=[consts_t])
            trk.op(POOL, lambda: nc.gpsimd.memset(mhalf[:], -0.5), writes=[consts_t])
            s0 = lamscr
            trk.op(DVE, lambda: nc.vector.tensor_tensor(out=s0[:, 512:640], in0=s0[:, 0:128], in1=s0[:, 128:256],
                                                        op=ALU.mult), reads=[consts_t], writes=[small_t])
            trk.op(DVE, lambda: nc.vector.tensor_tensor(out=s0[:, 640:768], in0=s0[:, 256:384], in1=s0[:, 384:512],
                                                        op=ALU.mult), reads=[consts_t], writes=[small_t])
            trk.op(DVE, lambda: nc.vector.tensor_reduce(out=small[:, 0:2],
                                                        in_=s0[:, 512:768].rearrange("p (a b) -> p a b", a=2),
                                                        axis=AX.X, op=ALU.add), reads=[small_t], writes=[small_t])
            trk.op(ACT, lambda: nc.scalar.activation(out=small[:, 2:4], in_=small[:, 0:2], func=AF.Exp),
                   reads=[small_t], writes=[small_t])
            trk.op(DVE, lambda: nc.vector.scalar_tensor_tensor(out=neglam[:], in0=small[:, 3:4], scalar=-LAM_INIT,
                                                               in1=small[:, 2:3], op0=ALU.add, op1=ALU.subtract),
                   reads=[small_t], writes=[small_t])
            trk.op(DVE, lambda: nc.vector.tensor_scalar(out=sgb[:], in0=sgb[:], scalar1=1.0 - LAM_INIT, scalar2=None,
                                                        op0=ALU.mult), reads=[consts_t], writes=[consts_t])
            wsf = arena[:].bitcast(F32)[:, 0:2048].rearrange("p (g s) -> p g s", g=16)
            wsb = arena[:, 4096:6144].rearrange("p (g s) -> p g s", g=16)
            ar_t = T("arena_setup")
            trk.dma(SP, ws_sem, wsf, a_w_s.rearrange("g t s -> t g s"), writes=[ar_t])
            trk.op(POOL, lambda: nc.gpsimd.affine_select(out=wsb, in_=wsf, pattern=[[0, 16], [-1, 128]],
                                                         compare_op=ALU.is_ge, fill=0.0, base=0,
                                                         channel_multiplier=1), reads=[ar_t], writes=[ar_t])
            for half in range(2):
                b = nbank()
                pv = banks[b][:].bitcast(BF16).rearrange("p (g t) -> p g t", t=128)
                for gi in range(8):
                    g = half * 8 + gi
                    trk.op(PE, lambda g=g, gi=gi: nc.tensor.transpose(out=pv[:, gi, :], in_=wsb[:, g, :],
                                                                       identity=ident[:]),
                           reads=[ar_t, consts_t], writes=[bank_t[b]], mark=(gi == 7))
                trk.op(DVE, lambda: nc.vector.tensor_copy(out=wsT[:, half * 8:(half + 1) * 8, :], in_=pv),
                       reads=[bank_t[b]], writes=[consts_t])
            ones_b = arena[:, 8192:8320]
            trk.op(POOL, lambda: nc.gpsimd.memset(ones_b, 1.0), writes=[ar_t])
            for q in range(4):
                b = nbank()
                trk.op(PE, lambda q=q: nc.tensor.matmul(out=banks[b][:], lhsT=ones_b,
                                                        rhs=wsT[:, q * 4:(q + 1) * 4, :].rearrange("p g t -> p (g t)"),
                                                        start=True, stop=True),
                       reads=[ar_t, consts_t], writes=[bank_t[b]])
                for gi in range(4):
                    g = q * 4 + gi
                    trk.op(DVE, lambda g=g, gi=gi: nc.vector.scalar_tensor_tensor(
                        out=biasT[:, g, :], in0=banks[b][:, gi * 128:(gi + 1) * 128], scalar=lnbT[:, g:g + 1],
                        in1=biasT[:, g, :], op0=ALU.mult, op1=ALU.add),
                        reads=[bank_t[b], consts_t], writes=[consts_t])

        def prepass():
            stg_f = [arena[:].bitcast(F32)[:, i * 4096:(i + 1) * 4096] for i in range(2)]
            stg_b = [arena[:, 16384 + i * 4096:16384 + (i + 1) * 4096] for i in range(2)]
            stf_t = [T("stf0"), T("stf1")]
            stb_t = [T("stb0"), T("stb1")]
            sem_in = [trk.new_sem("ppi0"), trk.new_sem("ppi1")]
            sem_out = [trk.new_sem("ppo0"), trk.new_sem("ppo1")]
            units = []
            for n in range(12):
                units.append((a_w_in.rearrange("(kc p) n -> p kc n", p=128)[:, :, n * 512:(n + 1) * 512], gA, 8, win_s[n]))
            for n in range(4):
                units.append((a_w_out.rearrange("(kc p) n -> p kc n", p=128)[:, :, n * 256:(n + 1) * 256], None, 16, wout_s[n]))
            for n in range(8):
                units.append((w_kv.rearrange("(kc p) n -> p kc n", p=128)[:, :, n * 512:(n + 1) * 512], gKV, 8, wkv_s[n]))
            for n in range(8):
                units.append((b_w_qz.rearrange("(kc p) n -> p kc n", p=128)[:, :, n * 512:(n + 1) * 512], gB, 8, wqz_s[n]))
            for n in range(4):
                units.append((b_w_o.rearrange("(kc p) n -> p kc n", p=128)[:, 4 * n:4 * n + 4, :], None, 4, wo_s[n]))
            def issue_in(ui):
                src, g, nk, dst = units[ui]
                i = ui % 2
                sf = stg_f[i].rearrange("p (k n) -> p k n", k=nk)
                trk.dma(SP, sem_in[i], sf, src, writes=[stf_t[i]])

            issue_in(0)
            for ui, (src, g, nk, dst) in enumerate(units):
                i = ui % 2
                sf = stg_f[i].rearrange("p (k n) -> p k n", k=nk)
                sbf = stg_b[i].rearrange("p (k n) -> p k n", k=nk)
                if ui + 1 < len(units):
                    issue_in(ui + 1)
                Eg = ACT if ui % 2 else DVE
                if g is None:
                    copy_on(Eg, stg_b[i], stg_f[i], [stf_t[i]], [stb_t[i]])
                else:
                    for k in range(nk):
                        if Eg is ACT:
                            trk.op(ACT, lambda k=k: nc.scalar.activation(out=sbf[:, k, :], in_=sf[:, k, :], func=AF.Copy,
                                                                         scale=g[:, k:k + 1]),
                                   reads=[stf_t[i], consts_t], writes=[stb_t[i]])
                        else:
                            trk.op(DVE, lambda k=k: nc.vector.tensor_scalar(out=sbf[:, k, :], in0=sf[:, k, :],
                                                                            scalar1=g[:, k:k + 1], scalar2=None,
                                                                            op0=ALU.mult),
                                   reads=[stf_t[i], consts_t], writes=[stb_t[i]])
                trk.dma(SP, sem_out[i], dst, stg_b[i], reads=[stb_t[i]])
            for i in range(2):
                SP.wait((sem_out[i], sem_out[i].count))

        def wload(slot_ids, dst_ap, src_ap):
            trk.dma(SP, wsl_sem[slot_ids[0]], dst_ap, src_ap, writes=[wsl_t[s] for s in slot_ids])

        def rms_stats(tiles, scol):
            n = len(tiles)
            for j, tt in enumerate(tiles):
                trk.op(ACT, lambda j=j, tt=tt: nc.scalar.activation(out=sqj[:], in_=h_sb[:, tt, :], func=AF.Square,
                                                                     accum_out=small[:, scol + j:scol + j + 1]),
                       reads=[h_t[tt]], writes=[sqj_t, small_t])
            trk.op(DVE, lambda: nc.vector.tensor_scalar(out=small[:, scol:scol + n], in0=small[:, scol:scol + n],
                                                        scalar1=1.0 / D, scalar2=EPS, op0=ALU.mult, op1=ALU.add),
                   reads=[small_t], writes=[small_t])
            trk.op(POOL, lambda: nc.gpsimd.tensor_tensor(out=small[:, scol:scol + n], in0=small[:, scol:scol + n],
                                                         in1=mhalf[:, 0:n], op=ALU.pow),
                   reads=[small_t, consts_t], writes=[small_t])

        def norm_transpose(tt, rcol, hT_ap, hT_tile):
            i = rr["hs"] % 2
            rr["hs"] += 1
            trk.op(ACT, lambda: nc.scalar.activation(out=hs[i][:], in_=h_sb[:, tt, :], func=AF.Copy,
                                                     scale=small[:, rcol:rcol + 1]),
                   reads=[h_t[tt], small_t], writes=[hs_t[i]])
            b = nbank()
            pv = banks[b][:].bitcast(BF16).rearrange("p (k t) -> p k t", t=128)
            for kc in range(8):
                trk.op(PE, lambda kc=kc: nc.tensor.transpose(out=pv[:, kc, :], in_=hs[i][:, kc * 128:(kc + 1) * 128],
                                                             identity=ident[:]),
                       reads=[hs_t[i], consts_t], writes=[bank_t[b]], mark=(kc == 7))
            copy_on(DVE, hT_ap, pv, [bank_t[b]], [hT_tile])

        hTa = [arena[:, i * 4096:(i + 1) * 4096].rearrange("p (k t) -> p k t", k=8) for i in range(2)]
        hTa_t = [T("hTa0"), T("hTa1")]
        gv = arena[:, 8192:16384].rearrange("p (j e) -> p j e", j=4)
        gv_t = [T(f"gv{j}") for j in range(4)]
        uT = arena[:, 16384:24576].rearrange("p (c t) -> p c t", c=16)
        uT_t = [T(f"uT{c}") for c in range(16)]
        bst = sb("bst", [128, 4, 4, 6], F32)
        bst_t = T("bst")
        mv = sb("mv", [128, 4, 2], F32)
        astate = {"slot": 0}

        def a_slot():
            k = astate["slot"] % 4
            astate["slot"] += 1
            return k

        def layer_a_block(blk, seq):
            tiles = [blk * 4 + j for j in range(4)]
            hT = hTa[blk % 2]
            hT_t = hTa_t[blk % 2]
            rms_stats(tiles, 0)
            for j, tt in enumerate(tiles):
                norm_transpose(tt, j, hT[:, :, j * 128:(j + 1) * 128], hT_t)

            def load_in(n):
                k = a_slot()
                dst = wsl[:, 2 * k:2 * k + 2, :].rearrange("p a n -> p (a n)")
                wload([2 * k, 2 * k + 1], dst, win_s[n])
                return k, dst.rearrange("p (k n) -> p k n", k=8)

            for n in range(4):
                k, w = load_in(4 + n)
                for j in range(4):
                    b = nbank()
                    for kc in range(8):
                        trk.op(PE, lambda kc=kc, j=j: nc.tensor.matmul(out=banks[b][:], lhsT=hT[:, kc, j * 128:(j + 1) * 128],
                                                                       rhs=w[:, kc, :], start=(kc == 0), stop=(kc == 7)),
                               reads=[hT_t, wsl_t[2 * k], wsl_t[2 * k + 1]], writes=[bank_t[b]], mark=(kc == 7))
                    trk.op(ACT, lambda j=j, n=n: nc.scalar.activation(out=gv[:, j, n * 512:(n + 1) * 512], in_=banks[b][:],
                                                                      func=AF.Gelu_apprx_tanh),
                           reads=[bank_t[b]], writes=[gv_t[j]])
                    trk.op(DVE, lambda j=j, n=n: nc.vector.bn_stats(out=bst[:, j, n, :], in_=gv[:, j, n * 512:(n + 1) * 512]),
                           reads=[gv_t[j]], writes=[bst_t])
            if blk + 1 < NBLK:
                load_x(seq, blk + 1)
            for j in range(4):
                trk.op(DVE, lambda j=j: nc.vector.bn_aggr(out=mv[:, j, :], in_=bst[:, j, :, :].rearrange("p a b -> p (a b)")),
                       reads=[bst_t], writes=[small_t])
            trk.op(DVE, lambda: nc.vector.tensor_scalar(out=small[:, 8:12], in0=mv[:, :, 1], scalar1=EPS, scalar2=None,
                                                        op0=ALU.add), reads=[small_t], writes=[small_t])
            trk.op(POOL, lambda: nc.gpsimd.tensor_tensor(out=small[:, 8:12], in0=small[:, 8:12], in1=mhalf[:, 0:4],
                                                         op=ALU.pow), reads=[small_t, consts_t], writes=[small_t])
            trk.op(DVE, lambda: nc.vector.scalar_tensor_tensor(out=small[:, 12:16], in0=mv[:, :, 0], scalar=-1.0,
                                                               in1=small[:, 8:12], op0=ALU.mult, op1=ALU.mult),
                   reads=[small_t], writes=[small_t])
            for j in range(4):
                trk.op(POOL, lambda j=j: nc.gpsimd.tensor_scalar(out=gv[:, j, :], in0=gv[:, j, :],
                                                                 scalar1=small[:, 8 + j:9 + j], scalar2=small[:, 12 + j:13 + j],
                                                                 op0=ALU.mult, op1=ALU.add),
                       reads=[gv_t[j], small_t], writes=[gv_t[j]])
            for n in range(4):
                k, w = load_in(n)
                for ci in range(4):
                    c = n * 4 + ci
                    b = nbank()
                    for kc in range(8):
                        trk.op(PE, lambda kc=kc, ci=ci: nc.tensor.matmul(out=banks[b][:], lhsT=w[:, kc, ci * 128:(ci + 1) * 128],
                                                                         rhs=hT[:, kc, :], start=(kc == 0), stop=(kc == 7)),
                               reads=[hT_t, wsl_t[2 * k], wsl_t[2 * k + 1]], writes=[bank_t[b]], mark=(kc == 7))
                    trk.op(ACT, lambda c=c: nc.scalar.activation(out=uT[:, c, :], in_=banks[b][:], func=AF.Gelu_apprx_tanh),
                           reads=[bank_t[b]], writes=[uT_t[c]])
            for n in range(4):
                k, w = load_in(8 + n)
                for ci in range(4):
                    c = n * 4 + ci
                    b = nbank()
                    for kc in range(8):
                        trk.op(PE, lambda kc=kc, ci=ci: nc.tensor.matmul(out=banks[b][:], lhsT=w[:, kc, ci * 128:(ci + 1) * 128],
                                                                         rhs=hT[:, kc, :], start=(kc == 0), stop=(kc == 7)),
                               reads=[hT_t, wsl_t[2 * k], wsl_t[2 * k + 1]], writes=[bank_t[b]], mark=(kc == 7))
                    ti = rr["tmpb"] % 3
                    rr["tmpb"] += 1
                    trk.op(ACT, lambda ti=ti: nc.scalar.activation(out=tmpb[ti], in_=banks[b][:], func=AF.Silu),
                           reads=[bank_t[b]], writes=[tmpb_t[ti]])
                    trk.op(DVE, lambda c=c, ti=ti: nc.vector.tensor_tensor(out=uT[:, c, :], in0=uT[:, c, :], in1=tmpb[ti],
                                                                           op=ALU.mult),
                           reads=[uT_t[c], tmpb_t[ti]], writes=[uT_t[c]])
            for g in range(16):
                b = nbank()
                for j in range(4):
                    trk.op(PE, lambda j=j, g=g: nc.tensor.matmul(out=banks[b][:, j * 128:(j + 1) * 128],
                                                                 lhsT=gv[:, j, g * 128:(g + 1) * 128], rhs=wsT[:, g, :],
                                                                 start=True, stop=True),
                           reads=[gv_t[j], consts_t], writes=[bank_t[b]], mark=(j == 3))
                ti = rr["tmpb"] % 3
                rr["tmpb"] += 1
                for j in range(4):
                    trk.op(DVE, lambda j=j, g=g, ti=ti: nc.vector.scalar_tensor_tensor(
                        out=tmpb[ti][:, j * 128:(j + 1) * 128], in0=banks[b][:, j * 128:(j + 1) * 128],
                        scalar=lngT[:, g:g + 1], in1=biasT[:, g, :], op0=ALU.mult, op1=ALU.add),
                        reads=[bank_t[b], consts_t], writes=[tmpb_t[ti]])
                trk.op(POOL, lambda g=g, ti=ti: nc.gpsimd.tensor_tensor(out=uT[:, g, :], in0=uT[:, g, :], in1=tmpb[ti],
                                                                        op=ALU.mult),
                       reads=[uT_t[g], tmpb_t[ti]], writes=[uT_t[g]])
            for dn in range(4):
                k = a_slot()
                dst = wsl[:, 2 * k:2 * k + 2, :].rearrange("p a n -> p (a n)")
                wload([2 * k, 2 * k + 1], dst, wout_s[dn])
                w = dst.rearrange("p (k n) -> p k n", k=16)
                for j, tt in enumerate(tiles):
                    b = nbank()
                    for kc in range(16):
                        trk.op(PE, lambda kc=kc, j=j: nc.tensor.matmul(out=banks[b][:, 0:256], lhsT=uT[:, kc, j * 128:(j + 1) * 128],
                                                                       rhs=w[:, kc, :], start=(kc == 0), stop=(kc == 15)),
                               reads=[uT_t[kc], wsl_t[2 * k], wsl_t[2 * k + 1]], writes=[bank_t[b]], mark=(kc == 15))
                    trk.op(DVE, lambda tt=tt, dn=dn: nc.vector.tensor_tensor(out=h_sb[:, tt, dn * 256:(dn + 1) * 256],
                                                                             in0=banks[b][:, 0:256],
                                                                             in1=h_sb[:, tt, dn * 256:(dn + 1) * 256], op=ALU.add),
                           reads=[bank_t[b], h_t[tt]], writes=[h_t[tt]])

        def final_stats(tt):
            rms_stats([tt], 32 + (tt % 8))

        def final_store(seq, tt, normalize):
            if normalize:
                c = 32 + (tt % 8)
                trk.op(DVE, lambda: nc.vector.scalar_tensor_tensor(out=h_sb[:, tt, :], in0=h_sb[:, tt, :], scalar=small[:, c:c + 1],
                                                                   in1=fgb[:], op0=ALU.mult, op1=ALU.mult),
                       reads=[h_t[tt], small_t, consts_t], writes=[h_t[tt]])
            trk.dma(SP, st_sem[(tt // 4) % 4], out[seq * SEQ + tt * 128: seq * SEQ + (tt + 1) * 128, :], h_sb[:, tt, :],
                    reads=[h_t[tt]])

        def load_x(seq, blk):
            src = x[seq * SEQ + blk * 512: seq * SEQ + (blk + 1) * 512, :].rearrange("(j p) d -> p j d", p=128)
            trk.dma(SP, x_sem[blk % 4], h_sb[:, blk * 4:(blk + 1) * 4, :], src,
                    writes=[h_t[blk * 4 + j] for j in range(4)])

        hTb = arena[:, 0:16384].rearrange("p (k t) -> p k t", k=8)
        hTb_t = [T(f"hTb{i}") for i in range(NT)]
        KT = arena[:, 16384:20480].rearrange("p (i t) -> p i t", i=2)
        KT_t = T("KT")
        Vb = arena[:, 20480:20480 + 16 * 264].rearrange("p (t e) -> p t e", t=16)
        V_t = T("V")
        zs = arena[:, 24704:28800].rearrange("p (t e) -> p t e", t=16)
        zs_t = T("zs")
        QTf = arena[:, 28800:32896].rearrange("p (i t) -> p i t", i=2)
        QT_t = [T(f"QT{i}") for i in range(NT // 2)]
        yTf = arena[:, 32896:36992].rearrange("p (c t) -> p c t", c=2)
        yT_t = [T(f"yT{i}") for i in range(NT)]
        NPT = 4
        PT = [arena[:, 36992 + i * 512:36992 + (i + 1) * 512].rearrange("p (i t) -> p i t", i=2) for i in range(NPT)]
        PT_t = [T(f"PT{i}") for i in range(NPT)]
        yb = [sb(f"yb{i}", [128, 256], BF16) for i in range(4)]
        yb_t = [T(f"yb{i}") for i in range(4)]
        dbuf = [sb(f"dbuf{i}", [128, 256], F32) for i in range(4)]
        dbuf_t = [T(f"dbuf{i}") for i in range(4)]
        fsm = sb("fsm", [128, 2, 8], F32)
        fsm_t = [T("fsm0"), T("fsm1")]
        bstate = {"slot": 0, "pt": 0, "y": 0}

        def b_slot():
            k = bstate["slot"] % 8
            bstate["slot"] += 1
            return k

        osb = sb("osb", [128, 4, 258], F32)
        osb_t = [T(f"osb{i}") for i in range(4)]
        sched = {"now": 0, "q": []}
        wo_pending = []

        def later(n, fn):
            sched["q"].append((sched["now"] + n, fn))

        def tick():
            sched["now"] += 1
            due = [e for e in sched["q"] if e[0] <= sched["now"]]
            sched["q"] = [e for e in sched["q"] if e[0] > sched["now"]]
            for _, fn in due:
                fn()

        def flush_all():
            while sched["q"]:
                tick()

        yt_ready = set()

        def pop_wo(n=1):
            for _ in range(n):
                if wo_pending and wo_pending[0][0] in yt_ready:
                    wo_pending.pop(0)[1]()

        def layer_b(seq):
            tiles = list(range(NT))
            NQB = NT // 2
            for t0 in range(0, NT, 4):
                rms_stats(tiles[t0:t0 + 4], 0)
                for j in range(4):
                    tt = t0 + j
                    norm_transpose(tt, j, hTb[:, :, tt * 128:(tt + 1) * 128], hTb_t[tt])
            allhT = hTb_t
            trk.op(POOL, lambda: nc.gpsimd.memset(Vb[:, :, 256:258], 1.0), writes=[V_t])
            for hd in range(8):
                ch, off = hd // 2, (hd % 2) * 256

                def load_w(src_s):
                    k = b_slot()
                    dst = wsl[:, k, :].rearrange("p (k n) -> p k n", k=8)
                    wload([k], dst, src_s[ch].rearrange("p (k n) -> p k n", k=8)[:, :, off:off + 256])
                    return k, dst

                kk, wk = load_w(wkv_s[0:4])
                kq, wq = load_w(wqz_s[0:4])
                kv_, wv = load_w(wkv_s[4:8])
                kz, wz = load_w(wqz_s[4:8])
                ko = b_slot()
                wo = wsl[:, ko, :].rearrange("p (c n) -> p c n", c=2)
                wload([ko], wo, wo_s[hd // 2].rearrange("p (a c n) -> p a c n", a=2, c=2)[:, hd % 2, :, :])
                pool_state["ids"] = list(range(8))
                for (wsrc, ksl, dstT, is_q) in ((wk, kk, KT, False), (wq, kq, QTf, True)):
                    for idx in range(2):
                        for tb in range(NBLK):
                            b = nbank()
                            for kc in range(8):
                                trk.op(PE, lambda kc=kc, idx=idx, tb=tb, b=b, wsrc=wsrc: nc.tensor.matmul(
                                    out=banks[b][:], lhsT=wsrc[:, kc, idx * 128:(idx + 1) * 128],
                                    rhs=hTb[:, kc, tb * 512:(tb + 1) * 512], start=(kc == 0), stop=(kc == 7)),
                                    reads=allhT[tb * 4:tb * 4 + 4] + [wsl_t[ksl]], writes=[bank_t[b]], mark=(kc == 7))
                            wr = [QT_t[2 * tb], QT_t[2 * tb + 1]] if is_q else [KT_t]
                            copy_on(ACT, dstT[:, idx, tb * 512:(tb + 1) * 512], banks[b][:], [bank_t[b]], wr)
                            tick()
                            pop_wo(1)
                flush_all()
                pop_wo(len(wo_pending))
                assert not wo_pending
                for tt in range(NT):
                    b = nbank()
                    for kc in range(8):
                        trk.op(PE, lambda kc=kc, tt=tt, b=b: nc.tensor.matmul(out=banks[b][:, 0:256],
                                                                              lhsT=hTb[:, kc, tt * 128:(tt + 1) * 128], rhs=wv[:, kc, :],
                                                                              start=(kc == 0), stop=(kc == 7)),
                               reads=[hTb_t[tt], wsl_t[kv_]], writes=[bank_t[b]], mark=(kc == 7))
                    copy_on(ACT, Vb[:, tt, 0:256], banks[b][:, 0:256], [bank_t[b]], [V_t])
                for tt in range(NT):
                    b = nbank()
                    for kc in range(8):
                        trk.op(PE, lambda kc=kc, tt=tt, b=b: nc.tensor.matmul(out=banks[b][:, 0:256],
                                                                              lhsT=hTb[:, kc, tt * 128:(tt + 1) * 128], rhs=wz[:, kc, :],
                                                                              start=(kc == 0), stop=(kc == 7)),
                               reads=[hTb_t[tt], wsl_t[kz]], writes=[bank_t[b]], mark=(kc == 7))
                    di = bstate["y"] % 2
                    bstate["y"] += 1
                    trk.op(ACT, lambda di=di, b=b: nc.scalar.activation(out=dbuf[di][:], in_=banks[b][:, 0:256], func=AF.Silu),
                           reads=[bank_t[b]], writes=[dbuf_t[di]])
                    trk.op(POOL, lambda di=di, tt=tt: nc.gpsimd.tensor_tensor(out=zs[:, tt, :], in0=dbuf[di][:], in1=sgb[:],
                                                                              op=ALU.mult),
                           reads=[dbuf_t[di], consts_t], writes=[zs_t])
                pool_state["ids"] = [7]
                SB = [4, 5, 6]

                def emit_s(qb, kt):
                    q0 = max(kt - 2 * qb, 0)
                    sbk = SB[bstate["pt"] % 3]
                    pi = bstate["pt"] % NPT
                    bstate["pt"] += 1
                    psv = banks[sbk][:].rearrange("p (i t) -> p i t", i=2)
                    for idx in range(2):
                        trk.op(PE, lambda idx=idx: nc.tensor.matmul(
                            out=psv[:, idx, q0 * 128:256], lhsT=KT[:, idx, kt * 128:(kt + 1) * 128],
                            rhs=QTf[:, idx, qb * 256 + q0 * 128:(qb + 1) * 256], start=True, stop=True),
                            reads=[KT_t, QT_t[qb]], writes=[bank_t[sbk]], mark=(idx == 1))
                    trk.op(ACT, lambda: nc.scalar.activation(out=PT[pi][:, :, q0 * 128:256], in_=psv[:, :, q0 * 128:256],
                                                             func=AF.Exp, scale=ATT_SCALE),
                           reads=[bank_t[sbk]], writes=[PT_t[pi]])
                    if kt >= 2 * qb:
                        dq = kt - 2 * qb
                        for idx in range(2):
                            trk.op(POOL, lambda idx=idx: nc.gpsimd.tensor_tensor(
                                out=PT[pi][:, idx, dq * 128:(dq + 1) * 128], in0=PT[pi][:, idx, dq * 128:(dq + 1) * 128],
                                in1=mask01[:], op=ALU.mult),
                                reads=[PT_t[pi], consts_t], writes=[PT_t[pi]])
                    return (pi, q0)

                def emit_pv(qb, kt, ctx):
                    pi, q0 = ctx
                    for qi in range(q0, 2):
                        for idx in range(2):
                            ob = qi * 2 + idx
                            trk.op(PE, lambda qi=qi, idx=idx, ob=ob: nc.tensor.matmul(
                                out=banks[ob][:, 0:258], lhsT=PT[pi][:, idx, qi * 128:(qi + 1) * 128], rhs=Vb[:, kt, 0:258],
                                start=(kt == 0), stop=(kt == 2 * qb + qi)),
                                reads=[PT_t[pi], V_t], writes=[bank_t[ob]], mark=True)

                def emit_fin(qb, hd_=hd):
                    par = qb % 2
                    ob, ob_t = osb, osb_t
                    fs, fs_t = fsm[:, par, :], fsm_t[par]
                    sets = [par * 2 + qi for qi in range(2)]

                    def st_evac():
                        for qi in range(2):
                            trk.op(ACT, lambda qi=qi: nc.scalar.activation(out=ob[:, qi * 2, :], in_=banks[qi * 2][:, 0:258], func=AF.Copy),
                                   reads=[bank_t[qi * 2]], writes=[ob_t[qi * 2]])
                            trk.op(DVE, lambda qi=qi: nc.vector.tensor_copy(out=ob[:, qi * 2 + 1, :], in_=banks[qi * 2 + 1][:, 0:258]),
                                   reads=[bank_t[qi * 2 + 1]], writes=[ob_t[qi * 2 + 1]])

                    def st_d():
                        trk.op(DVE, lambda: nc.vector.reciprocal(out=fs[:, 0:4], in_=ob[:, :, 256]), reads=ob_t, writes=[fs_t])
                        for qi in range(2):
                            trk.op(DVE, lambda qi=qi: nc.vector.tensor_tensor(out=fs[:, 2 * qi + 1:2 * qi + 2], in0=fs[:, 2 * qi + 1:2 * qi + 2],
                                                                              in1=neglam[:], op=ALU.mult),
                                   reads=[fs_t, small_t], writes=[fs_t])
                        for qi in range(2):
                            si = sets[qi]
                            trk.op(DVE, lambda qi=qi, si=si: nc.vector.tensor_scalar(out=dbuf[si][:], in0=ob[:, 2 * qi, 0:256],
                                                                                     scalar1=fs[:, 2 * qi:2 * qi + 1], scalar2=None,
                                                                                     op0=ALU.mult),
                                   reads=[ob_t[2 * qi], fs_t], writes=[dbuf_t[si]])
                            trk.op(DVE, lambda qi=qi, si=si: nc.vector.scalar_tensor_tensor(
                                out=dbuf[si][:], in0=ob[:, 2 * qi + 1, 0:256], scalar=fs[:, 2 * qi + 1:2 * qi + 2], in1=dbuf[si][:],
                                op0=ALU.mult, op1=ALU.add),
                                reads=[ob_t[2 * qi + 1], fs_t, dbuf_t[si]], writes=[dbuf_t[si]])

                    def st_ss():
                        for qi in range(2):
                            si = sets[qi]
                            trk.op(DVE, lambda qi=qi, si=si: nc.vector.scalar_tensor_tensor(
                                out=ob[:, 2 * qi, 0:256], in0=dbuf[si][:], scalar=1.0, in1=dbuf[si][:],
                                op0=ALU.mult, op1=ALU.mult, accum_out=fs[:, 4 + qi:5 + qi]),
                                reads=[dbuf_t[si]], writes=[ob_t[2 * qi], fs_t])
                        trk.op(DVE, lambda: nc.vector.tensor_scalar(out=fs[:, 4:6], in0=fs[:, 4:6], scalar1=1.0 / 256, scalar2=EPS,
                                                                    op0=ALU.mult, op1=ALU.add), reads=[fs_t], writes=[fs_t])
                        trk.op(POOL, lambda: nc.gpsimd.tensor_tensor(out=fs[:, 4:6], in0=fs[:, 4:6], in1=mhalf[:, 0:2], op=ALU.pow),
                               reads=[fs_t, consts_t], writes=[fs_t])

                    def st_y():
                        for qi in range(2):
                            si = sets[qi]
                            tt = 2 * qb + qi
                            trk.op(DVE, lambda qi=qi, si=si, tt=tt: nc.vector.scalar_tensor_tensor(
                                out=yb[si][:], in0=dbuf[si][:], scalar=fs[:, 4 + qi:5 + qi], in1=zs[:, tt, :],
                                op0=ALU.mult, op1=ALU.mult),
                                reads=[dbuf_t[si], fs_t, zs_t], writes=[yb_t[si]])

                    def st_tr():
                        for qi in range(2):
                            si = sets[qi]
                            tt = 2 * qb + qi
                            b = nbank()
                            pv = banks[b][:].bitcast(BF16)[:, 0:256].rearrange("p (c t) -> p c t", c=2)
                            for c in range(2):
                                trk.op(PE, lambda c=c, b=b, pv=pv, si=si: nc.tensor.transpose(
                                    out=pv[:, c, :], in_=yb[si][:, c * 128:(c + 1) * 128], identity=ident[:]),
                                    reads=[yb_t[si], consts_t], writes=[bank_t[b]], mark=(c == 1))
                            copy_on(DVE, yTf[:, :, tt * 128:(tt + 1) * 128], pv, [bank_t[b]], [yT_t[tt]])
                            yt_ready.add((seq, hd_, tt))

                    st_evac()
                    later(1, st_d)
                    later(3, st_ss)
                    later(8, st_y)
                    later(12, st_tr)

                def make_wo(tt, wo, ko, last_head):
                    def f():
                        for dc in range(2):
                            b = nbank()
                            for c in range(2):
                                trk.op(PE, lambda c=c, dc=dc, b=b: nc.tensor.matmul(
                                    out=banks[b][:], lhsT=yTf[:, c, tt * 128:(tt + 1) * 128], rhs=wo[:, c, dc * 512:(dc + 1) * 512],
                                    start=(c == 0), stop=(c == 1)),
                                    reads=[yT_t[tt], wsl_t[ko]], writes=[bank_t[b]], mark=(c == 1))
                            trk.op(DVE, lambda dc=dc, b=b: nc.vector.tensor_tensor(
                                out=h_sb[:, tt, dc * 512:(dc + 1) * 512], in0=banks[b][:],
                                in1=h_sb[:, tt, dc * 512:(dc + 1) * 512], op=ALU.add),
                                reads=[bank_t[b], h_t[tt]], writes=[h_t[tt]])
                    return f

                items = [(qb, kt) for qb in range(NQB) for kt in range(2 * qb + 2)]
                ctxs = {}
                for j in range(min(2, len(items))):
                    ctxs[j] = emit_s(*items[j])
                for i, (qb, kt) in enumerate(items):
                    if i + 2 < len(items):
                        ctxs[i + 2] = emit_s(*items[i + 2])
                    emit_pv(qb, kt, ctxs.pop(i))
                    tick()
                    if kt == 2 * qb + 1:
                        emit_fin(qb)
                pool_state["ids"] = list(range(8))
                if hd == 7:
                    flush_all()
                    _sync_all(trk, engines, SP)
                for tt in range(NT):
                    wo_pending.append(((seq, hd, tt), make_wo(tt, wo, ko, hd == 7)))
                if hd == 7:
                    for tt in range(NT + 2):
                        if tt < NT:
                            n0 = len(wo_pending)
                            pop_wo(1)
                            assert len(wo_pending) == n0 - 1
                            if final_norm:
                                final_stats(tt)
                        if tt >= 2:
                            final_store(seq, tt - 2, final_norm)
                    assert not wo_pending

        setup_consts()
        _sync_all(trk, engines, SP)
        prepass()
        _sync_all(trk, engines, SP)
        for seq in range(NSEQ):
            load_x(seq, 0)
            if do_a:
                for blk in range(NBLK):
                    layer_a_block(blk, seq)
                _sync_all(trk, engines, SP)
            else:
                for blk in range(1, NBLK):
                    load_x(seq, blk)
            if do_b:
                layer_b(seq)
            else:
                for tt in range(NT):
                    final_store(seq, tt, False)
                _sync_all(trk, engines, SP)
        for i in range(4):
            if st_sem[i].count:
                SP.wait((st_sem[i], st_sem[i].count))
                POOL.wait((st_sem[i], st_sem[i].count))
    return nc


def _sync_all(trk, engines, SP):
    for Ej in engines:
        if Ej.pending:
            raise RuntimeError("pending unmarked instruction at barrier on " + Ej.name)
    for Ei in engines + [SP]:
        for Ej in engines:
            if Ej is Ei or Ej.semw.count == 0:
                continue
            Ei.wait((Ej.semw, Ej.semw.count))


_NC_CACHE = {}


def _get_nc(key, **kw):
    if key not in _NC_CACHE:
        _NC_CACHE[key] = build(**kw)
    return _NC_CACHE[key]


def _param_map(inputs):
    f = lambda a: np.ascontiguousarray(np.asarray(a, dtype=np.float32))
    m = {
        "a_norm_g": f(inputs["a_norm_g"]).reshape(D),
        "a_w_in": f(inputs["a_w_in"]).reshape(D, 3 * E),
        "a_ln_g": f(inputs["a_ln_g"]).reshape(E),
        "a_ln_b": f(inputs["a_ln_b"]).reshape(E),
        "a_w_s": f(inputs["a_w_s"]).reshape(16, 128, 128),
        "a_b_s": f(inputs["a_b_s"]).reshape(16 * 128),
        "a_w_out": f(inputs["a_w_out"]).reshape(E, D),
        "b_norm_g": f(inputs["b_norm_g"]).reshape(D),
        "b_w_qz": f(inputs["b_w_qz"]).reshape(D, 2 * E),
        "b_lam_q1": f(inputs["b_lam_q1"]).reshape(128),
        "b_lam_k1": f(inputs["b_lam_k1"]).reshape(128),
        "b_lam_q2": f(inputs["b_lam_q2"]).reshape(128),
        "b_lam_k2": f(inputs["b_lam_k2"]).reshape(128),
        "b_subln_g": f(inputs["b_subln_g"]).reshape(256),
        "b_w_o": f(inputs["b_w_o"]).reshape(E, D),
        "kv_norm_g": f(inputs["kv_norm_g"]).reshape(D),
        "w_kv": f(inputs["w_kv"]).reshape(D, 2 * E),
        "final_g": f(inputs["final_g"]).reshape(D),
    }
    return m


def kernel(**inputs):
    x = np.ascontiguousarray(np.asarray(inputs["x"], dtype=np.float32))
    B, S, _ = x.shape
    per = B // NCORES
    nc = _get_nc(("full", per, S), NSEQ=per, SEQ=S)
    pm = _param_map(inputs)
    in_maps = []
    for c in range(NCORES):
        m = dict(pm)
        m["x"] = x[c * per:(c + 1) * per].reshape(per * S, D)
        in_maps.append(m)
    res = run_bass_kernel_spmd(nc, in_maps, core_ids=list(range(NCORES)))
    outs = [np.asarray(r["out"]).reshape(per, S, D) for r in res.results]
    return np.concatenate(outs, axis=0).astype(np.float32)
```

```python
import math
from contextlib import ExitStack
import numpy as np
import concourse.bass as bass
import concourse.mybir as mybir
from concourse.bass_utils import run_bass_kernel_spmd

F32 = mybir.dt.float32
BF16 = mybir.dt.bfloat16
AF = mybir.ActivationFunctionType
ALU = mybir.AluOpType
AX = mybir.AxisListType

D = 1024
E = 2048
EPS = 1e-6
NCORES = 8
LAM_INIT = 0.8 - 0.6 * math.exp(-0.3 * 1)
ATT_SCALE = 128 ** -0.5


class SemW:
    def __init__(self, h, name):
        self.h = h
        self.name = name
        self.count = 0


class T:
    def __init__(self, name, psum=False):
        self.name = name
        self.w = None
        self.r = {}
        self.psum = psum


class Eng:
    def __init__(self, trk, name, eng, compute=True):
        self.trk = trk
        self.name = name
        self.eng = eng
        self.compute = compute
        self.semw = None
        self.waited = {}
        self.pending = False
        if compute:
            self.new_epoch()

    def new_epoch(self):
        assert not self.pending
        self.semw = self.trk.new_sem(self.name)

    def wait(self, dep):
        semw, val = dep[0], dep[1]
        if self.waited.get(semw, 0) >= val:
            return
        self.eng.wait_ge(semw.h, val)
        self.waited[semw] = val


class Tracker:
    def __init__(self, nc, es):
        self.nc = nc
        self.es = es
        self.nsem = 0

    def new_sem(self, name):
        self.nsem += 1
        h = self.es.enter_context(self.nc.semaphore(f"s{self.nsem}_{name}"))
        return SemW(h, name)

    def _deps(self, E, reads, writes):
        deps = []
        for t in reads:
            if t.w is not None:
                deps.append((t.w, "raw"))
            if t.psum:
                for k, d in t.r.items():
                    if d[2] != E.name:
                        deps.append((d, "rar"))
        for t in writes:
            if t.w is not None:
                deps.append((t.w, "waw"))
            for d in t.r.values():
                deps.append((d, "war"))
        for d, kind in deps:
            if d[2] == E.name and d[0] is E.semw:
                if E.name == "pe" or kind != "raw":
                    continue
                if d[1] > d[0].count:
                    continue
            E.wait(d)

    def op(self, E, emit, reads=(), writes=(), mark=True):
        self._deps(E, reads, writes)
        inst = emit()
        if mark:
            E.semw.count += 1
            inst.then_inc(E.semw.h, 1)
            E.pending = False
            val = E.semw.count
        else:
            E.pending = True
            val = E.semw.count + 1
        dep = (E.semw, val, E.name)
        for t in reads:
            t.r[E.name] = dep
        for t in writes:
            t.w = dep
            t.r = {}
        return inst

    def dma(self, Q, semw, out, in_, reads=(), writes=(), **kw):
        self._deps(Q, reads, writes)
        inst = Q.eng.dma_start(out=out, in_=in_, **kw)
        inst.then_inc(semw.h, 16)
        semw.count += 16
        dep = (semw, semw.count, "dma")
        for t in reads:
            t.r["dma_" + semw.name] = dep
        for t in writes:
            t.w = dep
            t.r = {}
        return inst


def build(NSEQ=4, SEQ=2048, do_a=True, do_b=True, final_norm=True):
    nc = bass.Bass("TRN2", target_bir_lowering=False)
    NT = SEQ // 128
    NBLK = SEQ // 512
    NTOK = NSEQ * SEQ

    def din(name, shape):
        return nc.dram_tensor(name, list(shape), F32, kind="ExternalInput").ap()

    x = din("x", [NTOK, D])
    a_norm_g = din("a_norm_g", [D])
    a_w_in = din("a_w_in", [D, 3 * E])
    a_ln_g = din("a_ln_g", [E])
    a_ln_b = din("a_ln_b", [E])
    a_w_s = din("a_w_s", [16, 128, 128])
    a_b_s = din("a_b_s", [16 * 128])
    a_w_out = din("a_w_out", [E, D])
    b_norm_g = din("b_norm_g", [D])
    b_w_qz = din("b_w_qz", [D, 2 * E])
    b_lam = [din(n, [128]) for n in ("b_lam_q1", "b_lam_k1", "b_lam_q2", "b_lam_k2")]
    b_subln_g = din("b_subln_g", [256])
    b_w_o = din("b_w_o", [E, D])
    kv_norm_g = din("kv_norm_g", [D])
    w_kv = din("w_kv", [D, 2 * E])
    final_g = din("final_g", [D])
    out = nc.dram_tensor("out", [NTOK, D], F32, kind="ExternalOutput").ap()

    win_s = nc.dram_tensor("win_s", [12, 128, 4096], BF16).ap()
    wkv_s = nc.dram_tensor("wkv_s", [8, 128, 4096], BF16).ap()
    wqz_s = nc.dram_tensor("wqz_s", [8, 128, 4096], BF16).ap()
    wout_s = nc.dram_tensor("wout_s", [4, 128, 4096], BF16).ap()
    wo_s = nc.dram_tensor("wo_s", [4, 128, 4096], BF16).ap()

    es = ExitStack()
    with es:
        trk = Tracker(nc, es)
        PE = Eng(trk, "pe", nc.tensor)
        ACT = Eng(trk, "act", nc.scalar)
        DVE = Eng(trk, "dve", nc.vector)
        POOL = Eng(trk, "pool", nc.gpsimd)
        SP = Eng(trk, "sp", nc.sync, compute=False)
        engines = [PE, ACT, DVE, POOL]

        sbtot = {"b": 0}

        def sb(name, shape, dt):
            n = 1
            for d_ in shape[1:]:
                n *= d_
            sbtot["b"] += n * (4 if dt == F32 else 2)
            return es.enter_context(nc.sbuf_tensor(name, list(shape), dt))

        def barrier():
            for Ei in engines + [SP]:
                for Ej in engines:
                    if Ej is Ei or Ej.semw.count == 0:
                        continue
                    assert not Ej.pending
                    Ei.wait((Ej.semw, Ej.semw.count))

        h_sb = sb("h", [128, NT, D], F32)
        h_t = [T(f"h{i}") for i in range(NT)]
        wsl = sb("wsl", [128, 8, 2048], BF16)
        wsl_t = [T(f"wsl{i}") for i in range(8)]
        wsl_sem = [trk.new_sem(f"wsl{i}") for i in range(8)]
        arena = sb("arena", [128, 39040], BF16)
        ident = sb("ident", [128, 128], BF16)
        mask01 = sb("mask01", [128, 128], BF16)
        wsT = sb("wsT", [128, 16, 128], BF16)
        biasT = sb("biasT", [128, 16, 128], F32)
        lngT = sb("lngT", [128, 16], F32)
        lnbT = sb("lnbT", [128, 16], F32)
        gA = sb("gA", [128, 8], F32)
        gKV = sb("gKV", [128, 8], F32)
        gB = sb("gB", [128, 8], F32)
        sgb = sb("sgb", [128, 256], F32)
        fgb = sb("fgb", [128, D], F32)
        neglam = sb("neglam", [128, 1], F32)
        mhalf = sb("mhalf", [128, 16], F32)
        small = sb("small", [128, 64], F32)
        sqj = sb("sqj", [128, D], BF16)
        hs = [sb(f"hs{i}", [128, D], BF16) for i in range(2)]
        hs_t = [T(f"hs{i}") for i in range(2)]
        tmpb = [arena[:, 24576 + i * 512:24576 + (i + 1) * 512] for i in range(3)]
        tmpb_t = [T(f"tmpb{i}") for i in range(3)]
        st_sem = [trk.new_sem(f"st{i}") for i in range(4)]
        x_sem = [trk.new_sem(f"x{i}") for i in range(4)]
        c_sem = trk.new_sem("const")
        consts_t = T("consts")
        sqj_t = T("sqj")
        small_t = T("small")

        banks = [es.enter_context(nc.psum_tensor(f"bank{i}", [128, 512], F32)) for i in range(8)]
        bank_t = [T(f"bank{i}", psum=True) for i in range(8)]
        pool_state = {"ids": list(range(8)), "nxt": 0}

        def nbank():
            ids = pool_state["ids"]
            i = ids[pool_state["nxt"] % len(ids)]
            pool_state["nxt"] += 1
            return i

        rr = {"tmpb": 0, "ostg": 0, "hs": 0, "ev": 0}

        def evac_engine():
            rr["ev"] += 1
            return ACT if rr["ev"] % 2 else DVE

        def copy_on(Eg, out_ap, in_ap, reads, writes):
            if Eg is ACT:
                trk.op(ACT, lambda: nc.scalar.activation(out=out_ap, in_=in_ap, func=AF.Copy), reads, writes)
            else:
                trk.op(Eg, lambda: Eg.eng.tensor_copy(out=out_ap, in_=in_ap), reads, writes)

        lamscr = arena[:].bitcast(F32)[:, 5000:5768]
        cl_t = T("cl")
        ws_sem = trk.new_sem("wsld")

        def setup_consts():
            with nc.allow_non_contiguous_dma(reason="tiny one-time parameter loads"):
                def cl(dst, src):
                    trk.dma(SP, c_sem, dst, src, writes=[cl_t])
                cl(lngT[:], a_ln_g.rearrange("(g c) -> c g", c=128))
                cl(lnbT[:], a_ln_b.rearrange("(g c) -> c g", c=128))
                cl(gA[:], a_norm_g.rearrange("(k p) -> p k", p=128))
                cl(gKV[:], kv_norm_g.rearrange("(k p) -> p k", p=128))
                cl(gB[:], b_norm_g.rearrange("(k p) -> p k", p=128))
                cl(sgb[:], b_subln_g.partition_broadcast(128))
                cl(fgb[:], final_g.partition_broadcast(128))
                cl(biasT[:].rearrange("p g t -> p (g t)"), a_b_s.partition_broadcast(128))
                for i in range(4):
                    cl(lamscr[:, i * 128:(i + 1) * 128], b_lam[i].partition_broadcast(128))
            consts_t.w = (c_sem, c_sem.count, "dma")
            trk.op(POOL, lambda: nc.gpsimd.memset(ident[:], 0.0), writes=[consts_t])
            trk.op(POOL, lambda: nc.gpsimd.affine_select(out=ident[:], in_=ident[:], pattern=[[-1, 128]],
                                                         compare_op=ALU.not_equal, fill=1.0, base=0,
                                                         channel_multiplier=1), reads=[consts_t], writes=[consts_t])
            trk.op(POOL, lambda: nc.gpsimd.memset(mask01[:], 1.0), writes=[consts_t])
            trk.op(POOL, lambda: nc.gpsimd.affine_select(out=mask01[:], in_=mask01[:], pattern=[[1, 128]],
                                                         compare_op=ALU.is_ge, fill=0.0, base=0,
                                                         channel_multiplier=-1), reads=[consts_t], writes=[consts_t])
            trk.op(POOL, lambda: nc.gpsimd.memset(mhalf[:], -0.5), writes=[consts_t])
            s0 = lamscr
            trk.op(DVE, lambda: nc.vector.tensor_tensor(out=s0[:, 512:640], in0=s0[:, 0:128], in1=s0[:, 128:256],
                                                        op=ALU.mult), reads=[consts_t], writes=[small_t])
            trk.op(DVE, lambda: nc.vector.tensor_tensor(out=s0[:, 640:768], in0=s0[:, 256:384], in1=s0[:, 384:512],
                                                        op=ALU.mult), reads=[consts_t], writes=[small_t])
            trk.op(DVE, lambda: nc.vector.tensor_reduce(out=small[:, 0:2],
                                                        in_=s0[:, 512:768].rearrange("p (a b) -> p a b", a=2),
                                                        axis=AX.X, op=ALU.add), reads=[small_t], writes=[small_t])
            trk.op(ACT, lambda: nc.scalar.activation(out=small[:, 2:4], in_=small[:, 0:2], func=AF.Exp),
                   reads=[small_t], writes=[small_t])
            trk.op(DVE, lambda: nc.vector.scalar_tensor_tensor(out=neglam[:], in0=small[:, 3:4], scalar=-LAM_INIT,
                                                               in1=small[:, 2:3], op0=ALU.add, op1=ALU.subtract),
                   reads=[small_t], writes=[small_t])
            trk.op(DVE, lambda: nc.vector.tensor_scalar(out=sgb[:], in0=sgb[:], scalar1=1.0 - LAM_INIT, scalar2=None,
                                                        op0=ALU.mult), reads=[consts_t], writes=[consts_t])
            wsf = arena[:].bitcast(F32)[:, 0:2048].rearrange("p (g s) -> p g s", g=16)
            wsb = arena[:, 4096:6144].rearrange("p (g s) -> p g s", g=16)
            ar_t = T("arena_setup")
            trk.dma(SP, ws_sem, wsf, a_w_s.rearrange("g t s -> t g s"), writes=[ar_t])
            trk.op(POOL, lambda: nc.gpsimd.affine_select(out=wsb, in_=wsf, pattern=[[0, 16], [-1, 128]],
                                                         compare_op=ALU.is_ge, fill=0.0, base=0,
                                                         channel_multiplier=1), reads=[ar_t], writes=[ar_t])
            for half in range(2):
                b = nbank()
                pv = banks[b][:].bitcast(BF16).rearrange("p (g t) -> p g t", t=128)
                for gi in range(8):
                    g = half * 8 + gi
                    trk.op(PE, lambda g=g, gi=gi: nc.tensor.transpose(out=pv[:, gi, :], in_=wsb[:, g, :],
                                                                       identity=ident[:]),
                           reads=[ar_t, consts_t], writes=[bank_t[b]], mark=(gi == 7))
                trk.op(DVE, lambda: nc.vector.tensor_copy(out=wsT[:, half * 8:(half + 1) * 8, :], in_=pv),
                       reads=[bank_t[b]], writes=[consts_t])
            ones_b = arena[:, 8192:8320]
            trk.op(POOL, lambda: nc.gpsimd.memset(ones_b, 1.0), writes=[ar_t])
            for q in range(4):
                b = nbank()
                trk.op(PE, lambda q=q: nc.tensor.matmul(out=banks[b][:], lhsT=ones_b,
                                                        rhs=wsT[:, q * 4:(q + 1) * 4, :].rearrange("p g t -> p (g t)"),
                                                        start=True, stop=True),
                       reads=[ar_t, consts_t], writes=[bank_t[b]])
                for gi in range(4):
                    g = q * 4 + gi
                    trk.op(DVE, lambda g=g, gi=gi: nc.vector.scalar_tensor_tensor(
                        out=biasT[:, g, :], in0=banks[b][:, gi * 128:(gi + 1) * 128], scalar=lnbT[:, g:g + 1],
                        in1=biasT[:, g, :], op0=ALU.mult, op1=ALU.add),
                        reads=[bank_t[b], consts_t], writes=[consts_t])

        def prepass():
            stg_f = [arena[:].bitcast(F32)[:, i * 4096:(i + 1) * 4096] for i in range(2)]
            stg_b = [arena[:, 16384 + i * 4096:16384 + (i + 1) * 4096] for i in range(2)]
            stf_t = [T("stf0"), T("stf1")]
            stb_t = [T("stb0"), T("stb1")]
            sem_in = [trk.new_sem("ppi0"), trk.new_sem("ppi1")]
            sem_out = [trk.new_sem("ppo0"), trk.new_sem("ppo1")]
            units = []
            for n in range(12):
                units.append((a_w_in.rearrange("(kc p) n -> p kc n", p=128)[:, :, n * 512:(n + 1) * 512], gA, 8, win_s[n]))
            for n in range(4):
                units.append((a_w_out.rearrange("(kc p) n -> p kc n", p=128)[:, :, n * 256:(n + 1) * 256], None, 16, wout_s[n]))
            for n in range(8):
                units.append((w_kv.rearrange("(kc p) n -> p kc n", p=128)[:, :, n * 512:(n + 1) * 512], gKV, 8, wkv_s[n]))
            for n in range(8):
                units.append((b_w_qz.rearrange("(kc p) n -> p kc n", p=128)[:, :, n * 512:(n + 1) * 512], gB, 8, wqz_s[n]))
            for n in range(4):
                units.append((b_w_o.rearrange("(kc p) n -> p kc n", p=128)[:, 4 * n:4 * n + 4, :], None, 4, wo_s[n]))
            def issue_in(ui):
                src, g, nk, dst = units[ui]
                i = ui % 2
                sf = stg_f[i].rearrange("p (k n) -> p k n", k=nk)
                trk.dma(SP, sem_in[i], sf, src, writes=[stf_t[i]])

            issue_in(0)
            for ui, (src, g, nk, dst) in enumerate(units):
                i = ui % 2
                sf = stg_f[i].rearrange("p (k n) -> p k n", k=nk)
                sbf = stg_b[i].rearrange("p (k n) -> p k n", k=nk)
                if ui + 1 < len(units):
                    issue_in(ui + 1)
                Eg = ACT if ui % 2 else DVE
                if g is None:
                    copy_on(Eg, stg_b[i], stg_f[i], [stf_t[i]], [stb_t[i]])
                else:
                    for k in range(nk):
                        if Eg is ACT:
                            trk.op(ACT, lambda k=k: nc.scalar.activation(out=sbf[:, k, :], in_=sf[:, k, :], func=AF.Copy,
                                                                         scale=g[:, k:k + 1]),
                                   reads=[stf_t[i], consts_t], writes=[stb_t[i]])
                        else:
                            trk.op(DVE, lambda k=k: nc.vector.tensor_scalar(out=sbf[:, k, :], in0=sf[:, k, :],
                                                                            scalar1=g[:, k:k + 1], scalar2=None,
                                                                            op0=ALU.mult),
                                   reads=[stf_t[i], consts_t], writes=[stb_t[i]])
                trk.dma(SP, sem_out[i], dst, stg_b[i], reads=[stb_t[i]])
            for i in range(2):
                SP.wait((sem_out[i], sem_out[i].count))

        def wload(slot_ids, dst_ap, src_ap):
            trk.dma(SP, wsl_sem[slot_ids[0]], dst_ap, src_ap, writes=[wsl_t[s] for s in slot_ids])

        def rms_stats(tiles, scol):
            n = len(tiles)
            for j, tt in enumerate(tiles):
                trk.op(ACT, lambda j=j, tt=tt: nc.scalar.activation(out=sqj[:], in_=h_sb[:, tt, :], func=AF.Square,
                                                                     accum_out=small[:, scol + j:scol + j + 1]),
                       reads=[h_t[tt]], writes=[sqj_t, small_t])
            trk.op(DVE, lambda: nc.vector.tensor_scalar(out=small[:, scol:scol + n], in0=small[:, scol:scol + n],
                                                        scalar1=1.0 / D, scalar2=EPS, op0=ALU.mult, op1=ALU.add),
                   reads=[small_t], writes=[small_t])
            trk.op(POOL, lambda: nc.gpsimd.tensor_tensor(out=small[:, scol:scol + n], in0=small[:, scol:scol + n],
                                                         in1=mhalf[:, 0:n], op=ALU.pow),
                   reads=[small_t, consts_t], writes=[small_t])

        def norm_transpose(tt, rcol, hT_ap, hT_tile):
            i = rr["hs"] % 2
            rr["hs"] += 1
            trk.op(ACT, lambda: nc.scalar.activation(out=hs[i][:], in_=h_sb[:, tt, :], func=AF.Copy,
                                                     scale=small[:, rcol:rcol + 1]),
                   reads=[h_t[tt], small_t], writes=[hs_t[i]])
            b = nbank()
            pv = banks[b][:].bitcast(BF16).rearrange("p (k t) -> p k t", t=128)
            for kc in range(8):
                trk.op(PE, lambda kc=kc: nc.tensor.transpose(out=pv[:, kc, :], in_=hs[i][:, kc * 128:(kc + 1) * 128],
                                                             identity=ident[:]),
                       reads=[hs_t[i], consts_t], writes=[bank_t[b]], mark=(kc == 7))
            copy_on(DVE, hT_ap, pv, [bank_t[b]], [hT_tile])

        hTa = [arena[:, i * 4096:(i + 1) * 4096].rearrange("p (k t) -> p k t", k=8) for i in range(2)]
        hTa_t = [T("hTa0"), T("hTa1")]
        gv = arena[:, 8192:16384].rearrange("p (j e) -> p j e", j=4)
        gv_t = [T(f"gv{j}") for j in range(4)]
        uT = arena[:, 16384:24576].rearrange("p (c t) -> p c t", c=16)
        uT_t = [T(f"uT{c}") for c in range(16)]
        bst = sb("bst", [128, 4, 4, 6], F32)
        bst_t = T("bst")
        mv = sb("mv", [128, 4, 2], F32)
        astate = {"slot": 0}

        def a_slot():
            k = astate["slot"] % 4
            astate["slot"] += 1
            return k

        def layer_a_block(blk, seq):
            tiles = [blk * 4 + j for j in range(4)]
            hT = hTa[blk % 2]
            hT_t = hTa_t[blk % 2]
            rms_stats(tiles, 0)
            for j, tt in enumerate(tiles):
                norm_transpose(tt, j, hT[:, :, j * 128:(j + 1) * 128], hT_t)

            def load_in(n):
                k = a_slot()
                dst = wsl[:, 2 * k:2 * k + 2, :].rearrange("p a n -> p (a n)")
                wload([2 * k, 2 * k + 1], dst, win_s[n])
                return k, dst.rearrange("p (k n) -> p k n", k=8)

            for n in range(4):
                k, w = load_in(4 + n)
                for j in range(4):
                    b = nbank()
                    for kc in range(8):
                        trk.op(PE, lambda kc=kc, j=j: nc.tensor.matmul(out=banks[b][:], lhsT=hT[:, kc, j * 128:(j + 1) * 128],
                                                                       rhs=w[:, kc, :], start=(kc == 0), stop=(kc == 7)),
                               reads=[hT_t, wsl_t[2 * k], wsl_t[2 * k + 1]], writes=[bank_t[b]], mark=(kc == 7))
                    trk.op(ACT, lambda j=j, n=n: nc.scalar.activation(out=gv[:, j, n * 512:(n + 1) * 512], in_=banks[b][:],
                                                                      func=AF.Gelu_apprx_tanh),
                           reads=[bank_t[b]], writes=[gv_t[j]])
                    trk.op(DVE, lambda j=j, n=n: nc.vector.bn_stats(out=bst[:, j, n, :], in_=gv[:, j, n * 512:(n + 1) * 512]),
                           reads=[gv_t[j]], writes=[bst_t])
            if blk + 1 < NBLK:
                load_x(seq, blk + 1)
            for j in range(4):
                trk.op(DVE, lambda j=j: nc.vector.bn_aggr(out=mv[:, j, :], in_=bst[:, j, :, :].rearrange("p a b -> p (a b)")),
                       reads=[bst_t], writes=[small_t])
            trk.op(DVE, lambda: nc.vector.tensor_scalar(out=small[:, 8:12], in0=mv[:, :, 1], scalar1=EPS, scalar2=None,
                                                        op0=ALU.add), reads=[small_t], writes=[small_t])
            trk.op(POOL, lambda: nc.gpsimd.tensor_tensor(out=small[:, 8:12], in0=small[:, 8:12], in1=mhalf[:, 0:4],
                                                         op=ALU.pow), reads=[small_t, consts_t], writes=[small_t])
            trk.op(DVE, lambda: nc.vector.scalar_tensor_tensor(out=small[:, 12:16], in0=mv[:, :, 0], scalar=-1.0,
                                                               in1=small[:, 8:12], op0=ALU.mult, op1=ALU.mult),
                   reads=[small_t], writes=[small_t])
            for j in range(4):
                trk.op(POOL, lambda j=j: nc.gpsimd.tensor_scalar(out=gv[:, j, :], in0=gv[:, j, :],
                                                                 scalar1=small[:, 8 + j:9 + j], scalar2=small[:, 12 + j:13 + j],
                                                                 op0=ALU.mult, op1=ALU.add),
                       reads=[gv_t[j], small_t], writes=[gv_t[j]])
            for n in range(4):
                k, w = load_in(n)
                for ci in range(4):
                    c = n * 4 + ci
                    b = nbank()
                    for kc in range(8):
                        trk.op(PE, lambda kc=kc, ci=ci: nc.tensor.matmul(out=banks[b][:], lhsT=w[:, kc, ci * 128:(ci + 1) * 128],
                                                                         rhs=hT[:, kc, :], start=(kc == 0), stop=(kc == 7)),
                               reads=[hT_t, wsl_t[2 * k], wsl_t[2 * k + 1]], writes=[bank_t[b]], mark=(kc == 7))
                    trk.op(ACT, lambda c=c: nc.scalar.activation(out=uT[:, c, :], in_=banks[b][:], func=AF.Gelu_apprx_tanh),
                           reads=[bank_t[b]], writes=[uT_t[c]])
            for n in range(4):
                k, w = load_in(8 + n)
                for ci in range(4):
                    c = n * 4 + ci
                    b = nbank()
                    for kc in range(8):
                        trk.op(PE, lambda kc=kc, ci=ci: nc.tensor.matmul(out=banks[b][:], lhsT=w[:, kc, ci * 128:(ci + 1) * 128],
                                                                         rhs=hT[:, kc, :], start=(kc == 0), stop=(kc == 7)),
                               reads=[hT_t, wsl_t[2 * k], wsl_t[2 * k + 1]], writes=[bank_t[b]], mark=(kc == 7))
                    ti = rr["tmpb"] % 3
                    rr["tmpb"] += 1
                    trk.op(ACT, lambda ti=ti: nc.scalar.activation(out=tmpb[ti], in_=banks[b][:], func=AF.Silu),
                           reads=[bank_t[b]], writes=[tmpb_t[ti]])
                    trk.op(DVE, lambda c=c, ti=ti: nc.vector.tensor_tensor(out=uT[:, c, :], in0=uT[:, c, :], in1=tmpb[ti],
                                                                           op=ALU.mult),
                           reads=[uT_t[c], tmpb_t[ti]], writes=[uT_t[c]])
            for g in range(16):
                b = nbank()
                for j in range(4):
                    trk.op(PE, lambda j=j, g=g: nc.tensor.matmul(out=banks[b][:, j * 128:(j + 1) * 128],
                                                                 lhsT=gv[:, j, g * 128:(g + 1) * 128], rhs=wsT[:, g, :],
                                                                 start=True, stop=True),
                           reads=[gv_t[j], consts_t], writes=[bank_t[b]], mark=(j == 3))
                ti = rr["tmpb"] % 3
                rr["tmpb"] += 1
                for j in range(4):
                    trk.op(DVE, lambda j=j, g=g, ti=ti: nc.vector.scalar_tensor_tensor(
                        out=tmpb[ti][:, j * 128:(j + 1) * 128], in0=banks[b][:, j * 128:(j + 1) * 128],
                        scalar=lngT[:, g:g + 1], in1=biasT[:, g, :], op0=ALU.mult, op1=ALU.add),
                        reads=[bank_t[b], consts_t], writes=[tmpb_t[ti]])
                trk.op(POOL, lambda g=g, ti=ti: nc.gpsimd.tensor_tensor(out=uT[:, g, :], in0=uT[:, g, :], in1=tmpb[ti],
                                                                        op=ALU.mult),
                       reads=[uT_t[g], tmpb_t[ti]], writes=[uT_t[g]])
            for dn in range(4):
                k = a_slot()
                dst = wsl[:, 2 * k:2 * k + 2, :].rearrange("p a n -> p (a n)")
                wload([2 * k, 2 * k + 1], dst, wout_s[dn])
                w = dst.rearrange("p (k n) -> p k n", k=16)
                for j, tt in enumerate(tiles):
                    b = nbank()
                    for kc in range(16):
                        trk.op(PE, lambda kc=kc, j=j: nc.tensor.matmul(out=banks[b][:, 0:256], lhsT=uT[:, kc, j * 128:(j + 1) * 128],
                                                                       rhs=w[:, kc, :], start=(kc == 0), stop=(kc == 15)),
                               reads=[uT_t[kc], wsl_t[2 * k], wsl_t[2 * k + 1]], writes=[bank_t[b]], mark=(kc == 15))
                    trk.op(DVE, lambda tt=tt, dn=dn: nc.vector.tensor_tensor(out=h_sb[:, tt, dn * 256:(dn + 1) * 256],
                                                                             in0=banks[b][:, 0:256],
                                                                             in1=h_sb[:, tt, dn * 256:(dn + 1) * 256], op=ALU.add),
                           reads=[bank_t[b], h_t[tt]], writes=[h_t[tt]])

        def final_stats(tt):
            rms_stats([tt], 32 + (tt % 8))

        def final_store(seq, tt, normalize):
            if normalize:
                c = 32 + (tt % 8)
                trk.op(DVE, lambda: nc.vector.scalar_tensor_tensor(out=h_sb[:, tt, :], in0=h_sb[:, tt, :], scalar=small[:, c:c + 1],
                                                                   in1=fgb[:], op0=ALU.mult, op1=ALU.mult),
                       reads=[h_t[tt], small_t, consts_t], writes=[h_t[tt]])
            trk.dma(SP, st_sem[(tt // 4) % 4], out[seq * SEQ + tt * 128: seq * SEQ + (tt + 1) * 128, :], h_sb[:, tt, :],
                    reads=[h_t[tt]])

        def load_x(seq, blk):
            src = x[seq * SEQ + blk * 512: seq * SEQ + (blk + 1) * 512, :].rearrange("(j p) d -> p j d", p=128)
            trk.dma(SP, x_sem[blk % 4], h_sb[:, blk * 4:(blk + 1) * 4, :], src,
                    writes=[h_t[blk * 4 + j] for j in range(4)])

        hTb = arena[:, 0:16384].rearrange("p (k t) -> p k t", k=8)
        hTb_t = [T(f"hTb{i}") for i in range(NT)]
        KT = arena[:, 16384:20480].rearrange("p (i t) -> p i t", i=2)
        KT_t = T("KT")
        Vb = arena[:, 20480:20480 + 16 * 264].rearrange("p (t e) -> p t e", t=16)
        V_t = T("V")
        zs = arena[:, 24704:28800].rearrange("p (t e) -> p t e", t=16)
        zs_t = T("zs")
        QTf = arena[:, 28800:32896].rearrange("p (i t) -> p i t", i=2)
        QT_t = [T(f"QT{i}") for i in range(NT // 2)]
        yTf = arena[:, 32896:36992].rearrange("p (c t) -> p c t", c=2)
        yT_t = [T(f"yT{i}") for i in range(NT)]
        NPT = 4
        PT = [arena[:, 36992 + i * 512:36992 + (i + 1) * 512].rearrange("p (i t) -> p i t", i=2) for i in range(NPT)]
        PT_t = [T(f"PT{i}") for i in range(NPT)]
        yb = [sb(f"yb{i}", [128, 256], BF16) for i in range(4)]
        yb_t = [T(f"yb{i}") for i in range(4)]
        dbuf = [sb(f"dbuf{i}", [128, 256], F32) for i in range(4)]
        dbuf_t = [T(f"dbuf{i}") for i in range(4)]
        fsm = sb("fsm", [128, 2, 8], F32)
        fsm_t = [T("fsm0"), T("fsm1")]
        bstate = {"slot": 0, "pt": 0, "y": 0}

        def b_slot():
            k = bstate["slot"] % 8
            bstate["slot"] += 1
            return k

        osb = sb("osb", [128, 4, 258], F32)
        osb_t = [T(f"osb{i}") for i in range(4)]
        sched = {"now": 0, "q": []}
        wo_pending = []

        def later(n, fn):
            sched["q"].append((sched["now"] + n, fn))

        def tick():
            sched["now"] += 1
            due = [e for e in sched["q"] if e[0] <= sched["now"]]
            sched["q"] = [e for e in sched["q"] if e[0] > sched["now"]]
            for _, fn in due:
                fn()

        def flush_all():
            while sched["q"]:
                tick()

        yt_ready = set()

        def pop_wo(n=1):
            for _ in range(n):
                if wo_pending and wo_pending[0][0] in yt_ready:
                    wo_pending.pop(0)[1]()

        def layer_b(seq):
            tiles = list(range(NT))
            NQB = NT // 2
            for t0 in range(0, NT, 4):
                rms_stats(tiles[t0:t0 + 4], 0)
                for j in range(4):
                    tt = t0 + j
                    norm_transpose(tt, j, hTb[:, :, tt * 128:(tt + 1) * 128], hTb_t[tt])
            allhT = hTb_t
            trk.op(POOL, lambda: nc.gpsimd.memset(Vb[:, :, 256:258], 1.0), writes=[V_t])
            for hd in range(8):
                ch, off = hd // 2, (hd % 2) * 256

                def load_w(src_s):
                    k = b_slot()
                    dst = wsl[:, k, :].rearrange("p (k n) -> p k n", k=8)
                    wload([k], dst, src_s[ch].rearrange("p (k n) -> p k n", k=8)[:, :, off:off + 256])
                    return k, dst

                kk, wk = load_w(wkv_s[0:4])
                kq, wq = load_w(wqz_s[0:4])
                kv_, wv = load_w(wkv_s[4:8])
                kz, wz = load_w(wqz_s[4:8])
                ko = b_slot()
                wo = wsl[:, ko, :].rearrange("p (c n) -> p c n", c=2)
                wload([ko], wo, wo_s[hd // 2].rearrange("p (a c n) -> p a c n", a=2, c=2)[:, hd % 2, :, :])
                pool_state["ids"] = list(range(8))
                for (wsrc, ksl, dstT, is_q) in ((wk, kk, KT, False), (wq, kq, QTf, True)):
                    for idx in range(2):
                        for tb in range(NBLK):
                            b = nbank()
                            for kc in range(8):
                                trk.op(PE, lambda kc=kc, idx=idx, tb=tb, b=b, wsrc=wsrc: nc.tensor.matmul(
                                    out=banks[b][:], lhsT=wsrc[:, kc, idx * 128:(idx + 1) * 128],
                                    rhs=hTb[:, kc, tb * 512:(tb + 1) * 512], start=(kc == 0), stop=(kc == 7)),
                                    reads=allhT[tb * 4:tb * 4 + 4] + [wsl_t[ksl]], writes=[bank_t[b]], mark=(kc == 7))
                            wr = [QT_t[2 * tb], QT_t[2 * tb + 1]] if is_q else [KT_t]
                            copy_on(ACT, dstT[:, idx, tb * 512:(tb + 1) * 512], banks[b][:], [bank_t[b]], wr)
                            tick()
                            pop_wo(1)
                flush_all()
                pop_wo(len(wo_pending))
                assert not wo_pending
                for tt in range(NT):
                    b = nbank()
                    for kc in range(8):
                        trk.op(PE, lambda kc=kc, tt=tt, b=b: nc.tensor.matmul(out=banks[b][:, 0:256],
                                                                              lhsT=hTb[:, kc, tt * 128:(tt + 1) * 128], rhs=wv[:, kc, :],
                                                                              start=(kc == 0), stop=(kc == 7)),
                               reads=[hTb_t[tt], wsl_t[kv_]], writes=[bank_t[b]], mark=(kc == 7))
                    copy_on(ACT, Vb[:, tt, 0:256], banks[b][:, 0:256], [bank_t[b]], [V_t])
                for tt in range(NT):
                    b = nbank()
                    for kc in range(8):
                        trk.op(PE, lambda kc=kc, tt=tt, b=b: nc.tensor.matmul(out=banks[b][:, 0:256],
                                                                              lhsT=hTb[:, kc, tt * 128:(tt + 1) * 128], rhs=wz[:, kc, :],
                                                                              start=(kc == 0), stop=(kc == 7)),
                               reads=[hTb_t[tt], wsl_t[kz]], writes=[bank_t[b]], mark=(kc == 7))
                    di = bstate["y"] % 2
                    bstate["y"] += 1
                    trk.op(ACT, lambda di=di, b=b: nc.scalar.activation(out=dbuf[di][:], in_=banks[b][:, 0:256], func=AF.Silu),
                           reads=[bank_t[b]], writes=[dbuf_t[di]])
                    trk.op(POOL, lambda di=di, tt=tt: nc.gpsimd.tensor_tensor(out=zs[:, tt, :], in0=dbuf[di][:], in1=sgb[:],
                                                                              op=ALU.mult),
                           reads=[dbuf_t[di], consts_t], writes=[zs_t])
                pool_state["ids"] = [7]
                SB = [4, 5, 6]

                def emit_s(qb, kt):
                    q0 = max(kt - 2 * qb, 0)
                    sbk = SB[bstate["pt"] % 3]
                    pi = bstate["pt"] % NPT
                    bstate["pt"] += 1
                    psv = banks[sbk][:].rearrange("p (i t) -> p i t", i=2)
                    for idx in range(2):
                        trk.op(PE, lambda idx=idx: nc.tensor.matmul(
                            out=psv[:, idx, q0 * 128:256], lhsT=KT[:, idx, kt * 128:(kt + 1) * 128],
                            rhs=QTf[:, idx, qb * 256 + q0 * 128:(qb + 1) * 256], start=True, stop=True),
                            reads=[KT_t, QT_t[qb]], writes=[bank_t[sbk]], mark=(idx == 1))
                    trk.op(ACT, lambda: nc.scalar.activation(out=PT[pi][:, :, q0 * 128:256], in_=psv[:, :, q0 * 128:256],
                                                             func=AF.Exp, scale=ATT_SCALE),
                           reads=[bank_t[sbk]], writes=[PT_t[pi]])
                    if kt >= 2 * qb:
                        dq = kt - 2 * qb
                        for idx in range(2):
                            trk.op(POOL, lambda idx=idx: nc.gpsimd.tensor_tensor(
                                out=PT[pi][:, idx, dq * 128:(dq + 1) * 128], in0=PT[pi][:, idx, dq * 128:(dq + 1) * 128],
                                in1=mask01[:], op=ALU.mult),
                                reads=[PT_t[pi], consts_t], writes=[PT_t[pi]])
                    return (pi, q0)

                def emit_pv(qb, kt, ctx):
                    pi, q0 = ctx
                    for qi in range(q0, 2):
                        for idx in range(2):
                            ob = qi * 2 + idx
                            trk.op(PE, lambda qi=qi, idx=idx, ob=ob: nc.tensor.matmul(
                                out=banks[ob][:, 0:258], lhsT=PT[pi][:, idx, qi * 128:(qi + 1) * 128], rhs=Vb[:, kt, 0:258],
                                start=(kt == 0), stop=(kt == 2 * qb + qi)),
                                reads=[PT_t[pi], V_t], writes=[bank_t[ob]], mark=True)

                def emit_fin(qb, hd_=hd):
                    par = qb % 2
                    ob, ob_t = osb, osb_t
                    fs, fs_t = fsm[:, par, :], fsm_t[par]
                    sets = [par * 2 + qi for qi in range(2)]

                    def st_evac():
                        for qi in range(2):
                            trk.op(ACT, lambda qi=qi: nc.scalar.activation(out=ob[:, qi * 2, :], in_=banks[qi * 2][:, 0:258], func=AF.Copy),
                                   reads=[bank_t[qi * 2]], writes=[ob_t[qi * 2]])
                            trk.op(DVE, lambda qi=qi: nc.vector.tensor_copy(out=ob[:, qi * 2 + 1, :], in_=banks[qi * 2 + 1][:, 0:258]),
                                   reads=[bank_t[qi * 2 + 1]], writes=[ob_t[qi * 2 + 1]])

                    def st_d():
                        trk.op(DVE, lambda: nc.vector.reciprocal(out=fs[:, 0:4], in_=ob[:, :, 256]), reads=ob_t, writes=[fs_t])
                        for qi in range(2):
                            trk.op(DVE, lambda qi=qi: nc.vector.tensor_tensor(out=fs[:, 2 * qi + 1:2 * qi + 2], in0=fs[:, 2 * qi + 1:2 * qi + 2],
                                                                              in1=neglam[:], op=ALU.mult),
                                   reads=[fs_t, small_t], writes=[fs_t])
                        for qi in range(2):
                            si = sets[qi]
                            trk.op(DVE, lambda qi=qi, si=si: nc.vector.tensor_scalar(out=dbuf[si][:], in0=ob[:, 2 * qi, 0:256],
                                                                                     scalar1=fs[:, 2 * qi:2 * qi + 1], scalar2=None,
                                                                                     op0=ALU.mult),
                                   reads=[ob_t[2 * qi], fs_t], writes=[dbuf_t[si]])
                            trk.op(DVE, lambda qi=qi, si=si: nc.vector.scalar_tensor_tensor(
                                out=dbuf[si][:], in0=ob[:, 2 * qi + 1, 0:256], scalar=fs[:, 2 * qi + 1:2 * qi + 2], in1=dbuf[si][:],
                                op0=ALU.mult, op1=ALU.add),
                                reads=[ob_t[2 * qi + 1], fs_t, dbuf_t[si]], writes=[dbuf_t[si]])

                    def st_ss():
                        for qi in range(2):
                            si = sets[qi]
                            trk.op(DVE, lambda qi=qi, si=si: nc.vector.scalar_tensor_tensor(
                                out=ob[:, 2 * qi, 0:256], in0=dbuf[si][:], scalar=1.0, in1=dbuf[si][:],
                                op0=ALU.mult, op1=ALU.mult, accum_out=fs[:, 4 + qi:5 + qi]),
                                reads=[dbuf_t[si]], writes=[ob_t[2 * qi], fs_t])
                        trk.op(DVE, lambda: nc.vector.tensor_scalar(out=fs[:, 4:6], in0=fs[:, 4:6], scalar1=1.0 / 256, scalar2=EPS,
                                                                    op0=ALU.mult, op1=ALU.add), reads=[fs_t], writes=[fs_t])
                        trk.op(POOL, lambda: nc.gpsimd.tensor_tensor(out=fs[:, 4:6], in0=fs[:, 4:6], in1=mhalf[:, 0:2], op=ALU.pow),
                               reads=[fs_t, consts_t], writes=[fs_t])

                    def st_y():
                        for qi in range(2):
                            si = sets[qi]
                            tt = 2 * qb + qi
                            trk.op(DVE, lambda qi=qi, si=si, tt=tt: nc.vector.scalar_tensor_tensor(
                                out=yb[si][:], in0=dbuf[si][:], scalar=fs[:, 4 + qi:5 + qi], in1=zs[:, tt, :],
                                op0=ALU.mult, op1=ALU.mult),
                                reads=[dbuf_t[si], fs_t, zs_t], writes=[yb_t[si]])

                    def st_tr():
                        for qi in range(2):
                            si = sets[qi]
                            tt = 2 * qb + qi
                            b = nbank()
                            pv = banks[b][:].bitcast(BF16)[:, 0:256].rearrange("p (c t) -> p c t", c=2)
                            for c in range(2):
                                trk.op(PE, lambda c=c, b=b, pv=pv, si=si: nc.tensor.transpose(
                                    out=pv[:, c, :], in_=yb[si][:, c * 128:(c + 1) * 128], identity=ident[:]),
                                    reads=[yb_t[si], consts_t], writes=[bank_t[b]], mark=(c == 1))
                            copy_on(DVE, yTf[:, :, tt * 128:(tt + 1) * 128], pv, [bank_t[b]], [yT_t[tt]])
                            yt_ready.add((seq, hd_, tt))

                    st_evac()
                    later(1, st_d)
                    later(3, st_ss)
                    later(8, st_y)
                    later(12, st_tr)

                def make_wo(tt, wo, ko, last_head):
                    def f():
                        for dc in range(2):
                            b = nbank()
                            for c in range(2):
                                trk.op(PE, lambda c=c, dc=dc, b=b: nc.tensor.matmul(
                                    out=banks[b][:], lhsT=yTf[:, c, tt * 128:(tt + 1) * 128], rhs=wo[:, c, dc * 512:(dc + 1) * 512],
                                    start=(c == 0), stop=(c == 1)),
                                    reads=[yT_t[tt], wsl_t[ko]], writes=[bank_t[b]], mark=(c == 1))
                            trk.op(DVE, lambda dc=dc, b=b: nc.vector.tensor_tensor(
                                out=h_sb[:, tt, dc * 512:(dc + 1) * 512], in0=banks[b][:],
                                in1=h_sb[:, tt, dc * 512:(dc + 1) * 512], op=ALU.add),
                                reads=[bank_t[b], h_t[tt]], writes=[h_t[tt]])
                    return f

                items = [(qb, kt) for qb in range(NQB) for kt in range(2 * qb + 2)]
                ctxs = {}
                for j in range(min(2, len(items))):
                    ctxs[j] = emit_s(*items[j])
                for i, (qb, kt) in enumerate(items):
                    if i + 2 < len(items):
                        ctxs[i + 2] = emit_s(*items[i + 2])
                    emit_pv(qb, kt, ctxs.pop(i))
                    tick()
                    if kt == 2 * qb + 1:
                        emit_fin(qb)
                pool_state["ids"] = list(range(8))
                if hd == 7:
                    flush_all()
                    _sync_all(trk, engines, SP)
                for tt in range(NT):
                    wo_pending.append(((seq, hd, tt), make_wo(tt, wo, ko, hd == 7)))
                if hd == 7:
                    for tt in range(NT + 2):
                        if tt < NT:
                            n0 = len(wo_pending)
                            pop_wo(1)
                            assert len(wo_pending) == n0 - 1
                            if final_norm:
                                final_stats(tt)
                        if tt >= 2:
                            final_store(seq, tt - 2, final_norm)
                            if tt - 2 == 3 and seq + 1 < NSEQ:
                                load_x(seq + 1, 0)
                    assert not wo_pending

        setup_consts()
        _sync_all(trk, engines, SP)
        prepass()
        _sync_all(trk, engines, SP)
        for seq in range(NSEQ):
            if seq == 0 or not do_b:
                load_x(seq, 0)
            if do_a:
                for blk in range(NBLK):
                    layer_a_block(blk, seq)
                _sync_all(trk, engines, SP)
            else:
                for blk in range(1, NBLK):
                    load_x(seq, blk)
            if do_b:
                layer_b(seq)
            else:
                for tt in range(NT):
                    final_store(seq, tt, False)
                _sync_all(trk, engines, SP)
        for i in range(4):
            if st_sem[i].count:
                SP.wait((st_sem[i], st_sem[i].count))
                POOL.wait((st_sem[i], st_sem[i].count))
    return nc


def _sync_all(trk, engines, SP):
    for Ej in engines:
        if Ej.pending:
            raise RuntimeError("pending unmarked instruction at barrier on " + Ej.name)
    for Ei in engines + [SP]:
        for Ej in engines:
            if Ej is Ei or Ej.semw.count == 0:
                continue
            Ei.wait((Ej.semw, Ej.semw.count))


_NC_CACHE = {}


def _get_nc(key, **kw):
    if key not in _NC_CACHE:
        _NC_CACHE[key] = build(**kw)
    return _NC_CACHE[key]


def _param_map(inputs):
    f = lambda a: np.ascontiguousarray(np.asarray(a, dtype=np.float32))
    m = {
        "a_norm_g": f(inputs["a_norm_g"]).reshape(D),
        "a_w_in": f(inputs["a_w_in"]).reshape(D, 3 * E),
        "a_ln_g": f(inputs["a_ln_g"]).reshape(E),
        "a_ln_b": f(inputs["a_ln_b"]).reshape(E),
        "a_w_s": f(inputs["a_w_s"]).reshape(16, 128, 128),
        "a_b_s": f(inputs["a_b_s"]).reshape(16 * 128),
        "a_w_out": f(inputs["a_w_out"]).reshape(E, D),
        "b_norm_g": f(inputs["b_norm_g"]).reshape(D),
        "b_w_qz": f(inputs["b_w_qz"]).reshape(D, 2 * E),
        "b_lam_q1": f(inputs["b_lam_q1"]).reshape(128),
        "b_lam_k1": f(inputs["b_lam_k1"]).reshape(128),
        "b_lam_q2": f(inputs["b_lam_q2"]).reshape(128),
        "b_lam_k2": f(inputs["b_lam_k2"]).reshape(128),
        "b_subln_g": f(inputs["b_subln_g"]).reshape(256),
        "b_w_o": f(inputs["b_w_o"]).reshape(E, D),
        "kv_norm_g": f(inputs["kv_norm_g"]).reshape(D),
        "w_kv": f(inputs["w_kv"]).reshape(D, 2 * E),
        "final_g": f(inputs["final_g"]).reshape(D),
    }
    return m


def kernel(**inputs):
    x = np.ascontiguousarray(np.asarray(inputs["x"], dtype=np.float32))
    B, S, _ = x.shape
    per = B // NCORES
    nc = _get_nc(("full", per, S), NSEQ=per, SEQ=S)
    pm = _param_map(inputs)
    in_maps = []
    for c in range(NCORES):
        m = dict(pm)
        m["x"] = x[c * per:(c + 1) * per].reshape(per * S, D)
        in_maps.append(m)
    res = run_bass_kernel_spmd(nc, in_maps, core_ids=list(range(NCORES)))
    outs = [np.asarray(r["out"]).reshape(per, S, D) for r in res.results]
    return np.concatenate(outs, axis=0).astype(np.float32)
```

```python
import math
from contextlib import ExitStack
import numpy as np
import concourse.bass as bass
import concourse.mybir as mybir
from concourse.bass_utils import run_bass_kernel_spmd

F32 = mybir.dt.float32
BF16 = mybir.dt.bfloat16
AF = mybir.ActivationFunctionType
ALU = mybir.AluOpType
AX = mybir.AxisListType

D = 1024
E = 2048
EPS = 1e-6
NCORES = 8
LAM_INIT = 0.8 - 0.6 * math.exp(-0.3 * 1)
ATT_SCALE = 128 ** -0.5


class SemW:
    def __init__(self, h, name):
        self.h = h
        self.name = name
        self.count = 0


class T:
    def __init__(self, name, psum=False):
        self.name = name
        self.w = None
        self.r = {}
        self.psum = psum


class Eng:
    def __init__(self, trk, name, eng, compute=True):
        self.trk = trk
        self.name = name
        self.eng = eng
        self.compute = compute
        self.semw = None
        self.waited = {}
        self.pending = False
        if compute:
            self.new_epoch()

    def new_epoch(self):
        assert not self.pending
        self.semw = self.trk.new_sem(self.name)

    def wait(self, dep):
        semw, val = dep[0], dep[1]
        if self.waited.get(semw, 0) >= val:
            return
        self.eng.wait_ge(semw.h, val)
        self.waited[semw] = val


class Tracker:
    def __init__(self, nc, es):
        self.nc = nc
        self.es = es
        self.nsem = 0

    def new_sem(self, name):
        self.nsem += 1
        h = self.es.enter_context(self.nc.semaphore(f"s{self.nsem}_{name}"))
        return SemW(h, name)

    def _deps(self, E, reads, writes):
        deps = []
        for t in reads:
            if t.w is not None:
                deps.append((t.w, "raw"))
            if t.psum:
                for k, d in t.r.items():
                    if d[2] != E.name:
                        deps.append((d, "rar"))
        for t in writes:
            if t.w is not None:
                deps.append((t.w, "waw"))
            for d in t.r.values():
                deps.append((d, "war"))
        for d, kind in deps:
            if d[2] == E.name and d[0] is E.semw:
                if E.name == "pe" or kind != "raw":
                    continue
                if d[1] > d[0].count:
                    continue
            E.wait(d)

    def op(self, E, emit, reads=(), writes=(), mark=True):
        self._deps(E, reads, writes)
        inst = emit()
        if mark:
            E.semw.count += 1
            inst.then_inc(E.semw.h, 1)
            E.pending = False
            val = E.semw.count
        else:
            E.pending = True
            val = E.semw.count + 1
        dep = (E.semw, val, E.name)
        for t in reads:
            t.r[E.name] = dep
        for t in writes:
            t.w = dep
            t.r = {}
        return inst

    def dma(self, Q, semw, out, in_, reads=(), writes=(), **kw):
        self._deps(Q, reads, writes)
        inst = Q.eng.dma_start(out=out, in_=in_, **kw)
        inst.then_inc(semw.h, 16)
        semw.count += 16
        dep = (semw, semw.count, "dma")
        for t in reads:
            t.r["dma_" + semw.name] = dep
        for t in writes:
            t.w = dep
            t.r = {}
        return inst


def build(NSEQ=4, SEQ=2048, do_a=True, do_b=True, final_norm=True):
    nc = bass.Bass("TRN2", target_bir_lowering=False)
    NT = SEQ // 128
    NBLK = SEQ // 512
    NTOK = NSEQ * SEQ

    def din(name, shape):
        return nc.dram_tensor(name, list(shape), F32, kind="ExternalInput").ap()

    x = din("x", [NTOK, D])
    a_norm_g = din("a_norm_g", [D])
    a_w_in = din("a_w_in", [D, 3 * E])
    a_ln_g = din("a_ln_g", [E])
    a_ln_b = din("a_ln_b", [E])
    a_w_s = din("a_w_s", [16, 128, 128])
    a_b_s = din("a_b_s", [16 * 128])
    a_w_out = din("a_w_out", [E, D])
    b_norm_g = din("b_norm_g", [D])
    b_w_qz = din("b_w_qz", [D, 2 * E])
    b_lam = [din(n, [128]) for n in ("b_lam_q1", "b_lam_k1", "b_lam_q2", "b_lam_k2")]
    b_subln_g = din("b_subln_g", [256])
    b_w_o = din("b_w_o", [E, D])
    kv_norm_g = din("kv_norm_g", [D])
    w_kv = din("w_kv", [D, 2 * E])
    final_g = din("final_g", [D])
    out = nc.dram_tensor("out", [NTOK, D], F32, kind="ExternalOutput").ap()

    win_s = nc.dram_tensor("win_s", [12, 128, 4096], BF16).ap()
    wkv_s = nc.dram_tensor("wkv_s", [8, 128, 4096], BF16).ap()
    wqz_s = nc.dram_tensor("wqz_s", [8, 128, 4096], BF16).ap()
    wout_s = nc.dram_tensor("wout_s", [4, 128, 4096], BF16).ap()
    wo_s = nc.dram_tensor("wo_s", [4, 128, 4096], BF16).ap()

    es = ExitStack()
    with es:
        trk = Tracker(nc, es)
        PE = Eng(trk, "pe", nc.tensor)
        ACT = Eng(trk, "act", nc.scalar)
        DVE = Eng(trk, "dve", nc.vector)
        POOL = Eng(trk, "pool", nc.gpsimd)
        SP = Eng(trk, "sp", nc.sync, compute=False)
        engines = [PE, ACT, DVE, POOL]

        sbtot = {"b": 0}

        def sb(name, shape, dt):
            n = 1
            for d_ in shape[1:]:
                n *= d_
            sbtot["b"] += n * (4 if dt == F32 else 2)
            return es.enter_context(nc.sbuf_tensor(name, list(shape), dt))

        def barrier():
            for Ei in engines + [SP]:
                for Ej in engines:
                    if Ej is Ei or Ej.semw.count == 0:
                        continue
                    assert not Ej.pending
                    Ei.wait((Ej.semw, Ej.semw.count))

        h_sb = sb("h", [128, NT, D], F32)
        h_t = [T(f"h{i}") for i in range(NT)]
        wsl = sb("wsl", [128, 8, 2048], BF16)
        wsl_t = [T(f"wsl{i}") for i in range(8)]
        wsl_sem = [trk.new_sem(f"wsl{i}") for i in range(8)]
        arena = sb("arena", [128, 39040], BF16)
        ident = sb("ident", [128, 128], BF16)
        mask01 = sb("mask01", [128, 128], BF16)
        wsT = sb("wsT", [128, 16, 128], BF16)
        biasT = sb("biasT", [128, 16, 128], F32)
        lngT = sb("lngT", [128, 16], F32)
        lnbT = sb("lnbT", [128, 16], F32)
        gA = sb("gA", [128, 8], F32)
        gKV = sb("gKV", [128, 8], F32)
        gB = sb("gB", [128, 8], F32)
        sgb = sb("sgb", [128, 256], F32)
        fgb = sb("fgb", [128, D], F32)
        neglam = sb("neglam", [128, 1], F32)
        mhalf = sb("mhalf", [128, 16], F32)
        small = sb("small", [128, 64], F32)
        sqj = sb("sqj", [128, D], BF16)
        hs = [sb(f"hs{i}", [128, D], BF16) for i in range(2)]
        hs_t = [T(f"hs{i}") for i in range(2)]
        tmpb = [arena[:, 24576 + i * 512:24576 + (i + 1) * 512] for i in range(3)]
        tmpb_t = [T(f"tmpb{i}") for i in range(3)]
        st_sem = [trk.new_sem(f"st{i}") for i in range(4)]
        x_sem = [trk.new_sem(f"x{i}") for i in range(4)]
        c_sem = trk.new_sem("const")
        consts_t = T("consts")
        sqj_t = T("sqj")
        small_t = T("small")

        banks = [es.enter_context(nc.psum_tensor(f"bank{i}", [128, 512], F32)) for i in range(8)]
        bank_t = [T(f"bank{i}", psum=True) for i in range(8)]
        pool_state = {"ids": list(range(8)), "nxt": 0}

        def nbank():
            ids = pool_state["ids"]
            i = ids[pool_state["nxt"] % len(ids)]
            pool_state["nxt"] += 1
            return i

        rr = {"tmpb": 0, "ostg": 0, "hs": 0, "ev": 0}

        def evac_engine():
            rr["ev"] += 1
            return ACT if rr["ev"] % 2 else DVE

        def copy_on(Eg, out_ap, in_ap, reads, writes):
            if Eg is ACT:
                trk.op(ACT, lambda: nc.scalar.activation(out=out_ap, in_=in_ap, func=AF.Copy), reads, writes)
            else:
                trk.op(Eg, lambda: Eg.eng.tensor_copy(out=out_ap, in_=in_ap), reads, writes)

        lamscr = arena[:].bitcast(F32)[:, 5000:5768]
        cl_t = T("cl")
        ws_sem = trk.new_sem("wsld")

        def setup_consts():
            with nc.allow_non_contiguous_dma(reason="tiny one-time parameter loads"):
                def cl(dst, src):
                    trk.dma(SP, c_sem, dst, src, writes=[cl_t])
                cl(lngT[:], a_ln_g.rearrange("(g c) -> c g", c=128))
                cl(lnbT[:], a_ln_b.rearrange("(g c) -> c g", c=128))
                cl(gA[:], a_norm_g.rearrange("(k p) -> p k", p=128))
                cl(gKV[:], kv_norm_g.rearrange("(k p) -> p k", p=128))
                cl(gB[:], b_norm_g.rearrange("(k p) -> p k", p=128))
                cl(sgb[:], b_subln_g.partition_broadcast(128))
                cl(fgb[:], final_g.partition_broadcast(128))
                cl(biasT[:].rearrange("p g t -> p (g t)"), a_b_s.partition_broadcast(128))
                for i in range(4):
                    cl(lamscr[:, i * 128:(i + 1) * 128], b_lam[i].partition_broadcast(128))
            consts_t.w = (c_sem, c_sem.count, "dma")
            trk.op(POOL, lambda: nc.gpsimd.memset(ident[:], 0.0), writes=[consts_t])
            trk.op(POOL, lambda: nc.gpsimd.affine_select(out=ident[:], in_=ident[:], pattern=[[-1, 128]],
                                                         compare_op=ALU.not_equal, fill=1.0, base=0,
                                                         channel_multiplier=1), reads=[consts_t], writes=[consts_t])
            trk.op(POOL, lambda: nc.gpsimd.memset(mask01[:], 1.0), writes=[consts_t])
            trk.op(POOL, lambda: nc.gpsimd.affine_select(out=mask01[:], in_=mask01[:], pattern=[[1, 128]],
                                                         compare_op=ALU.is_ge, fill=0.0, base=0,
                                                         channel_multiplier=-1), reads=[consts_t], writes=[consts_t])
            trk.op(POOL, lambda: nc.gpsimd.memset(mhalf[:], -0.5), writes=[consts_t])
            s0 = lamscr
            trk.op(DVE, lambda: nc.vector.tensor_tensor(out=s0[:, 512:640], in0=s0[:, 0:128], in1=s0[:, 128:256],
                                                        op=ALU.mult), reads=[consts_t], writes=[small_t])
            trk.op(DVE, lambda: nc.vector.tensor_tensor(out=s0[:, 640:768], in0=s0[:, 256:384], in1=s0[:, 384:512],
                                                        op=ALU.mult), reads=[consts_t], writes=[small_t])
            trk.op(DVE, lambda: nc.vector.tensor_reduce(out=small[:, 0:2],
                                                        in_=s0[:, 512:768].rearrange("p (a b) -> p a b", a=2),
                                                        axis=AX.X, op=ALU.add), reads=[small_t], writes=[small_t])
            trk.op(ACT, lambda: nc.scalar.activation(out=small[:, 2:4], in_=small[:, 0:2], func=AF.Exp),
                   reads=[small_t], writes=[small_t])
            trk.op(DVE, lambda: nc.vector.scalar_tensor_tensor(out=neglam[:], in0=small[:, 3:4], scalar=-LAM_INIT,
                                                               in1=small[:, 2:3], op0=ALU.add, op1=ALU.subtract),
                   reads=[small_t], writes=[small_t])
            trk.op(DVE, lambda: nc.vector.tensor_scalar(out=sgb[:], in0=sgb[:], scalar1=1.0 - LAM_INIT, scalar2=None,
                                                        op0=ALU.mult), reads=[consts_t], writes=[consts_t])
            wsf = arena[:].bitcast(F32)[:, 0:2048].rearrange("p (g s) -> p g s", g=16)
            wsb = arena[:, 4096:6144].rearrange("p (g s) -> p g s", g=16)
            ar_t = T("arena_setup")
            trk.dma(SP, ws_sem, wsf, a_w_s.rearrange("g t s -> t g s"), writes=[ar_t])
            trk.op(POOL, lambda: nc.gpsimd.affine_select(out=wsb, in_=wsf, pattern=[[0, 16], [-1, 128]],
                                                         compare_op=ALU.is_ge, fill=0.0, base=0,
                                                         channel_multiplier=1), reads=[ar_t], writes=[ar_t])
            for half in range(2):
                b = nbank()
                pv = banks[b][:].bitcast(BF16).rearrange("p (g t) -> p g t", t=128)
                for gi in range(8):
                    g = half * 8 + gi
                    trk.op(PE, lambda g=g, gi=gi: nc.tensor.transpose(out=pv[:, gi, :], in_=wsb[:, g, :],
                                                                       identity=ident[:]),
                           reads=[ar_t, consts_t], writes=[bank_t[b]], mark=(gi == 7))
                trk.op(DVE, lambda: nc.vector.tensor_copy(out=wsT[:, half * 8:(half + 1) * 8, :], in_=pv),
                       reads=[bank_t[b]], writes=[consts_t])
            ones_b = arena[:, 8192:8320]
            trk.op(POOL, lambda: nc.gpsimd.memset(ones_b, 1.0), writes=[ar_t])
            for q in range(4):
                b = nbank()
                trk.op(PE, lambda q=q: nc.tensor.matmul(out=banks[b][:], lhsT=ones_b,
                                                        rhs=wsT[:, q * 4:(q + 1) * 4, :].rearrange("p g t -> p (g t)"),
                                                        start=True, stop=True),
                       reads=[ar_t, consts_t], writes=[bank_t[b]])
                for gi in range(4):
                    g = q * 4 + gi
                    trk.op(DVE, lambda g=g, gi=gi: nc.vector.scalar_tensor_tensor(
                        out=biasT[:, g, :], in0=banks[b][:, gi * 128:(gi + 1) * 128], scalar=lnbT[:, g:g + 1],
                        in1=biasT[:, g, :], op0=ALU.mult, op1=ALU.add),
                        reads=[bank_t[b], consts_t], writes=[consts_t])

        def prepass():
            stg_f = [arena[:].bitcast(F32)[:, i * 4096:(i + 1) * 4096] for i in range(2)]
            stg_b = [arena[:, 16384 + i * 4096:16384 + (i + 1) * 4096] for i in range(2)]
            stf_t = [T("stf0"), T("stf1")]
            stb_t = [T("stb0"), T("stb1")]
            sem_in = [trk.new_sem("ppi0"), trk.new_sem("ppi1")]
            sem_out = [trk.new_sem("ppo0"), trk.new_sem("ppo1")]
            units = []
            for n in range(12):
                units.append((a_w_in.rearrange("(kc p) n -> p kc n", p=128)[:, :, n * 512:(n + 1) * 512], gA, 8, win_s[n]))
            for n in range(4):
                units.append((a_w_out.rearrange("(kc p) n -> p kc n", p=128)[:, :, n * 256:(n + 1) * 256], None, 16, wout_s[n]))
            for n in range(8):
                units.append((w_kv.rearrange("(kc p) n -> p kc n", p=128)[:, :, n * 512:(n + 1) * 512], gKV, 8, wkv_s[n]))
            for n in range(8):
                units.append((b_w_qz.rearrange("(kc p) n -> p kc n", p=128)[:, :, n * 512:(n + 1) * 512], gB, 8, wqz_s[n]))
            for n in range(4):
                units.append((b_w_o.rearrange("(kc p) n -> p kc n", p=128)[:, 4 * n:4 * n + 4, :], None, 4, wo_s[n]))
            def issue_in(ui):
                src, g, nk, dst = units[ui]
                i = ui % 2
                sf = stg_f[i].rearrange("p (k n) -> p k n", k=nk)
                trk.dma(SP, sem_in[i], sf, src, writes=[stf_t[i]])

            issue_in(0)
            for ui, (src, g, nk, dst) in enumerate(units):
                i = ui % 2
                sf = stg_f[i].rearrange("p (k n) -> p k n", k=nk)
                sbf = stg_b[i].rearrange("p (k n) -> p k n", k=nk)
                if ui + 1 < len(units):
                    issue_in(ui + 1)
                Eg = ACT if ui % 2 else DVE
                if g is None:
                    copy_on(Eg, stg_b[i], stg_f[i], [stf_t[i]], [stb_t[i]])
                else:
                    for k in range(nk):
                        if Eg is ACT:
                            trk.op(ACT, lambda k=k: nc.scalar.activation(out=sbf[:, k, :], in_=sf[:, k, :], func=AF.Copy,
                                                                         scale=g[:, k:k + 1]),
                                   reads=[stf_t[i], consts_t], writes=[stb_t[i]])
                        else:
                            trk.op(DVE, lambda k=k: nc.vector.tensor_scalar(out=sbf[:, k, :], in0=sf[:, k, :],
                                                                            scalar1=g[:, k:k + 1], scalar2=None,
                                                                            op0=ALU.mult),
                                   reads=[stf_t[i], consts_t], writes=[stb_t[i]])
                trk.dma(SP, sem_out[i], dst, stg_b[i], reads=[stb_t[i]])
            for i in range(2):
                SP.wait((sem_out[i], sem_out[i].count))

        def wload(slot_ids, dst_ap, src_ap):
            trk.dma(SP, wsl_sem[slot_ids[0]], dst_ap, src_ap, writes=[wsl_t[s] for s in slot_ids])

        def rms_stats(tiles, scol):
            n = len(tiles)
            for j, tt in enumerate(tiles):
                trk.op(ACT, lambda j=j, tt=tt: nc.scalar.activation(out=sqj[:], in_=h_sb[:, tt, :], func=AF.Square,
                                                                     accum_out=small[:, scol + j:scol + j + 1]),
                       reads=[h_t[tt]], writes=[sqj_t, small_t])
            trk.op(DVE, lambda: nc.vector.tensor_scalar(out=small[:, scol:scol + n], in0=small[:, scol:scol + n],
                                                        scalar1=1.0 / D, scalar2=EPS, op0=ALU.mult, op1=ALU.add),
                   reads=[small_t], writes=[small_t])
            trk.op(POOL, lambda: nc.gpsimd.tensor_tensor(out=small[:, scol:scol + n], in0=small[:, scol:scol + n],
                                                         in1=mhalf[:, 0:n], op=ALU.pow),
                   reads=[small_t, consts_t], writes=[small_t])

        def norm_transpose(tt, rcol, hT_ap, hT_tile):
            i = rr["hs"] % 2
            rr["hs"] += 1
            trk.op(ACT, lambda: nc.scalar.activation(out=hs[i][:], in_=h_sb[:, tt, :], func=AF.Copy,
                                                     scale=small[:, rcol:rcol + 1]),
                   reads=[h_t[tt], small_t], writes=[hs_t[i]])
            b = nbank()
            pv = banks[b][:].bitcast(BF16).rearrange("p (k t) -> p k t", t=128)
            for kc in range(8):
                trk.op(PE, lambda kc=kc: nc.tensor.transpose(out=pv[:, kc, :], in_=hs[i][:, kc * 128:(kc + 1) * 128],
                                                             identity=ident[:]),
                       reads=[hs_t[i], consts_t], writes=[bank_t[b]], mark=(kc == 7))
            copy_on(DVE, hT_ap, pv, [bank_t[b]], [hT_tile])

        hTa = [arena[:, i * 4096:(i + 1) * 4096].rearrange("p (k t) -> p k t", k=8) for i in range(2)]
        hTa_t = [T("hTa0"), T("hTa1")]
        gv = arena[:, 8192:16384].rearrange("p (j e) -> p j e", j=4)
        gv_t = [T(f"gv{j}") for j in range(4)]
        uT = arena[:, 16384:24576].rearrange("p (c t) -> p c t", c=16)
        uT_t = [T(f"uT{c}") for c in range(16)]
        bst = sb("bst", [128, 4, 4, 6], F32)
        bst_t = T("bst")
        mv = sb("mv", [128, 4, 2], F32)
        astate = {"slot": 0}

        def a_slot():
            k = astate["slot"] % 4
            astate["slot"] += 1
            return k

        a_pref = []

        def issue_in(n):
            k = a_slot()
            dst = wsl[:, 2 * k:2 * k + 2, :].rearrange("p a n -> p (a n)")
            wload([2 * k, 2 * k + 1], dst, win_s[n])
            return k, dst.rearrange("p (k n) -> p k n", k=8)

        def prefetch_a(ns):
            for n in ns:
                a_pref.append((n,) + issue_in(n))

        def layer_a_block(blk, seq):
            tiles = [blk * 4 + j for j in range(4)]
            hT = hTa[blk % 2]
            hT_t = hTa_t[blk % 2]
            rms_stats(tiles, 0)
            for j, tt in enumerate(tiles):
                norm_transpose(tt, j, hT[:, :, j * 128:(j + 1) * 128], hT_t)

            def load_in(n):
                if a_pref:
                    n0, k, w = a_pref.pop(0)
                    assert n0 == n
                    return k, w
                return issue_in(n)

            for n in range(4):
                k, w = load_in(4 + n)
                for j in range(4):
                    b = nbank()
                    for kc in range(8):
                        trk.op(PE, lambda kc=kc, j=j: nc.tensor.matmul(out=banks[b][:], lhsT=hT[:, kc, j * 128:(j + 1) * 128],
                                                                       rhs=w[:, kc, :], start=(kc == 0), stop=(kc == 7)),
                               reads=[hT_t, wsl_t[2 * k], wsl_t[2 * k + 1]], writes=[bank_t[b]], mark=(kc == 7))
                    trk.op(ACT, lambda j=j, n=n: nc.scalar.activation(out=gv[:, j, n * 512:(n + 1) * 512], in_=banks[b][:],
                                                                      func=AF.Gelu_apprx_tanh),
                           reads=[bank_t[b]], writes=[gv_t[j]])
                    trk.op(DVE, lambda j=j, n=n: nc.vector.bn_stats(out=bst[:, j, n, :], in_=gv[:, j, n * 512:(n + 1) * 512]),
                           reads=[gv_t[j]], writes=[bst_t])
            if blk + 1 < NBLK:
                load_x(seq, blk + 1)
            for j in range(4):
                trk.op(DVE, lambda j=j: nc.vector.bn_aggr(out=mv[:, j, :], in_=bst[:, j, :, :].rearrange("p a b -> p (a b)")),
                       reads=[bst_t], writes=[small_t])
            trk.op(DVE, lambda: nc.vector.tensor_scalar(out=small[:, 8:12], in0=mv[:, :, 1], scalar1=EPS, scalar2=None,
                                                        op0=ALU.add), reads=[small_t], writes=[small_t])
            trk.op(POOL, lambda: nc.gpsimd.tensor_tensor(out=small[:, 8:12], in0=small[:, 8:12], in1=mhalf[:, 0:4],
                                                         op=ALU.pow), reads=[small_t, consts_t], writes=[small_t])
            trk.op(DVE, lambda: nc.vector.scalar_tensor_tensor(out=small[:, 12:16], in0=mv[:, :, 0], scalar=-1.0,
                                                               in1=small[:, 8:12], op0=ALU.mult, op1=ALU.mult),
                   reads=[small_t], writes=[small_t])
            for j in range(4):
                trk.op(POOL, lambda j=j: nc.gpsimd.tensor_scalar(out=gv[:, j, :], in0=gv[:, j, :],
                                                                 scalar1=small[:, 8 + j:9 + j], scalar2=small[:, 12 + j:13 + j],
                                                                 op0=ALU.mult, op1=ALU.add),
                       reads=[gv_t[j], small_t], writes=[gv_t[j]])
            for n in range(4):
                k, w = load_in(n)
                for ci in range(4):
                    c = n * 4 + ci
                    b = nbank()
                    for kc in range(8):
                        trk.op(PE, lambda kc=kc, ci=ci: nc.tensor.matmul(out=banks[b][:], lhsT=w[:, kc, ci * 128:(ci + 1) * 128],
                                                                         rhs=hT[:, kc, :], start=(kc == 0), stop=(kc == 7)),
                               reads=[hT_t, wsl_t[2 * k], wsl_t[2 * k + 1]], writes=[bank_t[b]], mark=(kc == 7))
                    trk.op(ACT, lambda c=c: nc.scalar.activation(out=uT[:, c, :], in_=banks[b][:], func=AF.Gelu_apprx_tanh),
                           reads=[bank_t[b]], writes=[uT_t[c]])
            for n in range(4):
                k, w = load_in(8 + n)
                for ci in range(4):
                    c = n * 4 + ci
                    b = nbank()
                    for kc in range(8):
                        trk.op(PE, lambda kc=kc, ci=ci: nc.tensor.matmul(out=banks[b][:], lhsT=w[:, kc, ci * 128:(ci + 1) * 128],
                                                                         rhs=hT[:, kc, :], start=(kc == 0), stop=(kc == 7)),
                               reads=[hT_t, wsl_t[2 * k], wsl_t[2 * k + 1]], writes=[bank_t[b]], mark=(kc == 7))
                    ti = rr["tmpb"] % 3
                    rr["tmpb"] += 1
                    trk.op(ACT, lambda ti=ti: nc.scalar.activation(out=tmpb[ti], in_=banks[b][:], func=AF.Silu),
                           reads=[bank_t[b]], writes=[tmpb_t[ti]])
                    trk.op(DVE, lambda c=c, ti=ti: nc.vector.tensor_tensor(out=uT[:, c, :], in0=uT[:, c, :], in1=tmpb[ti],
                                                                           op=ALU.mult),
                           reads=[uT_t[c], tmpb_t[ti]], writes=[uT_t[c]])
            for g in range(16):
                b = nbank()
                for j in range(4):
                    trk.op(PE, lambda j=j, g=g: nc.tensor.matmul(out=banks[b][:, j * 128:(j + 1) * 128],
                                                                 lhsT=gv[:, j, g * 128:(g + 1) * 128], rhs=wsT[:, g, :],
                                                                 start=True, stop=True),
                           reads=[gv_t[j], consts_t], writes=[bank_t[b]], mark=(j == 3))
                ti = rr["tmpb"] % 3
                rr["tmpb"] += 1
                for j in range(4):
                    trk.op(DVE, lambda j=j, g=g, ti=ti: nc.vector.scalar_tensor_tensor(
                        out=tmpb[ti][:, j * 128:(j + 1) * 128], in0=banks[b][:, j * 128:(j + 1) * 128],
                        scalar=lngT[:, g:g + 1], in1=biasT[:, g, :], op0=ALU.mult, op1=ALU.add),
                        reads=[bank_t[b], consts_t], writes=[tmpb_t[ti]])
                trk.op(POOL, lambda g=g, ti=ti: nc.gpsimd.tensor_tensor(out=uT[:, g, :], in0=uT[:, g, :], in1=tmpb[ti],
                                                                        op=ALU.mult),
                       reads=[uT_t[g], tmpb_t[ti]], writes=[uT_t[g]])
            for dn in range(4):
                k = a_slot()
                dst = wsl[:, 2 * k:2 * k + 2, :].rearrange("p a n -> p (a n)")
                wload([2 * k, 2 * k + 1], dst, wout_s[dn])
                w = dst.rearrange("p (k n) -> p k n", k=16)
                for j, tt in enumerate(tiles):
                    b = nbank()
                    for kc in range(16):
                        trk.op(PE, lambda kc=kc, j=j: nc.tensor.matmul(out=banks[b][:, 0:256], lhsT=uT[:, kc, j * 128:(j + 1) * 128],
                                                                       rhs=w[:, kc, :], start=(kc == 0), stop=(kc == 15)),
                               reads=[uT_t[kc], wsl_t[2 * k], wsl_t[2 * k + 1]], writes=[bank_t[b]], mark=(kc == 15))
                    trk.op(DVE, lambda tt=tt, dn=dn: nc.vector.tensor_tensor(out=h_sb[:, tt, dn * 256:(dn + 1) * 256],
                                                                             in0=banks[b][:, 0:256],
                                                                             in1=h_sb[:, tt, dn * 256:(dn + 1) * 256], op=ALU.add),
                           reads=[bank_t[b], h_t[tt]], writes=[h_t[tt]])

        def final_stats(tt):
            rms_stats([tt], 32 + (tt % 8))

        def final_store(seq, tt, normalize):
            if normalize:
                c = 32 + (tt % 8)
                trk.op(DVE, lambda: nc.vector.scalar_tensor_tensor(out=h_sb[:, tt, :], in0=h_sb[:, tt, :], scalar=small[:, c:c + 1],
                                                                   in1=fgb[:], op0=ALU.mult, op1=ALU.mult),
                       reads=[h_t[tt], small_t, consts_t], writes=[h_t[tt]])
            trk.dma(SP, st_sem[(tt // 4) % 4], out[seq * SEQ + tt * 128: seq * SEQ + (tt + 1) * 128, :], h_sb[:, tt, :],
                    reads=[h_t[tt]])

        def load_x(seq, blk):
            src = x[seq * SEQ + blk * 512: seq * SEQ + (blk + 1) * 512, :].rearrange("(j p) d -> p j d", p=128)
            trk.dma(SP, x_sem[blk % 4], h_sb[:, blk * 4:(blk + 1) * 4, :], src,
                    writes=[h_t[blk * 4 + j] for j in range(4)])

        hTb = arena[:, 0:16384].rearrange("p (k t) -> p k t", k=8)
        hTb_t = [T(f"hTb{i}") for i in range(NT)]
        KT = arena[:, 16384:20480].rearrange("p (i t) -> p i t", i=2)
        KT_t = T("KT")
        Vb = arena[:, 20480:20480 + 16 * 264].rearrange("p (t e) -> p t e", t=16)
        V_t = T("V")
        zs = arena[:, 24704:28800].rearrange("p (t e) -> p t e", t=16)
        zs_t = T("zs")
        QTf = arena[:, 28800:32896].rearrange("p (i t) -> p i t", i=2)
        QT_t = [T(f"QT{i}") for i in range(NT // 2)]
        yTf = arena[:, 32896:36992].rearrange("p (c t) -> p c t", c=2)
        yT_t = [T(f"yT{i}") for i in range(NT)]
        NPT = 4
        PT = [arena[:, 36992 + i * 512:36992 + (i + 1) * 512].rearrange("p (i t) -> p i t", i=2) for i in range(NPT)]
        PT_t = [T(f"PT{i}") for i in range(NPT)]
        yb = [sb(f"yb{i}", [128, 256], BF16) for i in range(4)]
        yb_t = [T(f"yb{i}") for i in range(4)]
        dbuf = [sb(f"dbuf{i}", [128, 256], F32) for i in range(4)]
        dbuf_t = [T(f"dbuf{i}") for i in range(4)]
        fsm = sb("fsm", [128, 2, 8], F32)
        fsm_t = [T("fsm0"), T("fsm1")]
        bstate = {"slot": 0, "pt": 0, "y": 0}

        def b_slot():
            k = bstate["slot"] % 8
            bstate["slot"] += 1
            return k

        osb = sb("osb", [128, 4, 258], F32)
        osb_t = [T(f"osb{i}") for i in range(4)]
        sched = {"now": 0, "q": []}
        wo_pending = []

        def later(n, fn):
            sched["q"].append((sched["now"] + n, fn))

        def tick():
            sched["now"] += 1
            due = [e for e in sched["q"] if e[0] <= sched["now"]]
            sched["q"] = [e for e in sched["q"] if e[0] > sched["now"]]
            for _, fn in due:
                fn()

        def flush_all():
            while sched["q"]:
                tick()

        yt_ready = set()

        def pop_wo(n=1):
            for _ in range(n):
                if wo_pending and wo_pending[0][0] in yt_ready:
                    wo_pending.pop(0)[1]()

        def layer_b(seq):
            tiles = list(range(NT))
            NQB = NT // 2
            for t0 in range(0, NT, 4):
                rms_stats(tiles[t0:t0 + 4], 0)
                for j in range(4):
                    tt = t0 + j
                    norm_transpose(tt, j, hTb[:, :, tt * 128:(tt + 1) * 128], hTb_t[tt])
            allhT = hTb_t
            trk.op(POOL, lambda: nc.gpsimd.memset(Vb[:, :, 256:258], 1.0), writes=[V_t])
            for hd in range(8):
                ch, off = hd // 2, (hd % 2) * 256

                def load_w(src_s):
                    k = b_slot()
                    dst = wsl[:, k, :].rearrange("p (k n) -> p k n", k=8)
                    wload([k], dst, src_s[ch].rearrange("p (k n) -> p k n", k=8)[:, :, off:off + 256])
                    return k, dst

                kk, wk = load_w(wkv_s[0:4])
                kq, wq = load_w(wqz_s[0:4])
                kv_, wv = load_w(wkv_s[4:8])
                kz, wz = load_w(wqz_s[4:8])
                ko = b_slot()
                wo = wsl[:, ko, :].rearrange("p (c n) -> p c n", c=2)
                wload([ko], wo, wo_s[hd // 2].rearrange("p (a c n) -> p a c n", a=2, c=2)[:, hd % 2, :, :])
                pool_state["ids"] = list(range(8))
                for (wsrc, ksl, dstT, is_q) in ((wk, kk, KT, False), (wq, kq, QTf, True)):
                    for idx in range(2):
                        for tb in range(NBLK):
                            b = nbank()
                            for kc in range(8):
                                trk.op(PE, lambda kc=kc, idx=idx, tb=tb, b=b, wsrc=wsrc: nc.tensor.matmul(
                                    out=banks[b][:], lhsT=wsrc[:, kc, idx * 128:(idx + 1) * 128],
                                    rhs=hTb[:, kc, tb * 512:(tb + 1) * 512], start=(kc == 0), stop=(kc == 7)),
                                    reads=allhT[tb * 4:tb * 4 + 4] + [wsl_t[ksl]], writes=[bank_t[b]], mark=(kc == 7))
                            wr = [QT_t[2 * tb], QT_t[2 * tb + 1]] if is_q else [KT_t]
                            copy_on(ACT, dstT[:, idx, tb * 512:(tb + 1) * 512], banks[b][:], [bank_t[b]], wr)
                            tick()
                            pop_wo(1)
                flush_all()
                pop_wo(len(wo_pending))
                assert not wo_pending
                for tt in range(NT):
                    b = nbank()
                    for kc in range(8):
                        trk.op(PE, lambda kc=kc, tt=tt, b=b: nc.tensor.matmul(out=banks[b][:, 0:256],
                                                                              lhsT=hTb[:, kc, tt * 128:(tt + 1) * 128], rhs=wv[:, kc, :],
                                                                              start=(kc == 0), stop=(kc == 7)),
                               reads=[hTb_t[tt], wsl_t[kv_]], writes=[bank_t[b]], mark=(kc == 7))
                    copy_on(ACT, Vb[:, tt, 0:256], banks[b][:, 0:256], [bank_t[b]], [V_t])
                for tt in range(NT):
                    b = nbank()
                    for kc in range(8):
                        trk.op(PE, lambda kc=kc, tt=tt, b=b: nc.tensor.matmul(out=banks[b][:, 0:256],
                                                                              lhsT=hTb[:, kc, tt * 128:(tt + 1) * 128], rhs=wz[:, kc, :],
                                                                              start=(kc == 0), stop=(kc == 7)),
                               reads=[hTb_t[tt], wsl_t[kz]], writes=[bank_t[b]], mark=(kc == 7))
                    di = bstate["y"] % 2
                    bstate["y"] += 1
                    trk.op(ACT, lambda di=di, b=b: nc.scalar.activation(out=dbuf[di][:], in_=banks[b][:, 0:256], func=AF.Silu),
                           reads=[bank_t[b]], writes=[dbuf_t[di]])
                    trk.op(POOL, lambda di=di, tt=tt: nc.gpsimd.tensor_tensor(out=zs[:, tt, :], in0=dbuf[di][:], in1=sgb[:],
                                                                              op=ALU.mult),
                           reads=[dbuf_t[di], consts_t], writes=[zs_t])
                pool_state["ids"] = [7]
                SB = [4, 5, 6]

                def emit_s(qb, kt):
                    q0 = max(kt - 2 * qb, 0)
                    sbk = SB[bstate["pt"] % 3]
                    pi = bstate["pt"] % NPT
                    bstate["pt"] += 1
                    psv = banks[sbk][:].rearrange("p (i t) -> p i t", i=2)
                    for idx in range(2):
                        trk.op(PE, lambda idx=idx: nc.tensor.matmul(
                            out=psv[:, idx, q0 * 128:256], lhsT=KT[:, idx, kt * 128:(kt + 1) * 128],
                            rhs=QTf[:, idx, qb * 256 + q0 * 128:(qb + 1) * 256], start=True, stop=True),
                            reads=[KT_t, QT_t[qb]], writes=[bank_t[sbk]], mark=(idx == 1))
                    trk.op(ACT, lambda: nc.scalar.activation(out=PT[pi][:, :, q0 * 128:256], in_=psv[:, :, q0 * 128:256],
                                                             func=AF.Exp, scale=ATT_SCALE),
                           reads=[bank_t[sbk]], writes=[PT_t[pi]])
                    if kt >= 2 * qb:
                        dq = kt - 2 * qb
                        for idx in range(2):
                            trk.op(POOL, lambda idx=idx: nc.gpsimd.tensor_tensor(
                                out=PT[pi][:, idx, dq * 128:(dq + 1) * 128], in0=PT[pi][:, idx, dq * 128:(dq + 1) * 128],
                                in1=mask01[:], op=ALU.mult),
                                reads=[PT_t[pi], consts_t], writes=[PT_t[pi]])
                    return (pi, q0)

                def emit_pv(qb, kt, ctx):
                    pi, q0 = ctx
                    for qi in range(q0, 2):
                        for idx in range(2):
                            ob = qi * 2 + idx
                            trk.op(PE, lambda qi=qi, idx=idx, ob=ob: nc.tensor.matmul(
                                out=banks[ob][:, 0:258], lhsT=PT[pi][:, idx, qi * 128:(qi + 1) * 128], rhs=Vb[:, kt, 0:258],
                                start=(kt == 0), stop=(kt == 2 * qb + qi)),
                                reads=[PT_t[pi], V_t], writes=[bank_t[ob]], mark=True)

                def emit_fin(qb, hd_=hd):
                    par = qb % 2
                    ob, ob_t = osb, osb_t
                    fs, fs_t = fsm[:, par, :], fsm_t[par]
                    sets = [par * 2 + qi for qi in range(2)]

                    def st_evac():
                        for qi in range(2):
                            trk.op(ACT, lambda qi=qi: nc.scalar.activation(out=ob[:, qi * 2, :], in_=banks[qi * 2][:, 0:258], func=AF.Copy),
                                   reads=[bank_t[qi * 2]], writes=[ob_t[qi * 2]])
                            trk.op(DVE, lambda qi=qi: nc.vector.tensor_copy(out=ob[:, qi * 2 + 1, :], in_=banks[qi * 2 + 1][:, 0:258]),
                                   reads=[bank_t[qi * 2 + 1]], writes=[ob_t[qi * 2 + 1]])

                    def st_d():
                        trk.op(DVE, lambda: nc.vector.reciprocal(out=fs[:, 0:4], in_=ob[:, :, 256]), reads=ob_t, writes=[fs_t])
                        for qi in range(2):
                            trk.op(DVE, lambda qi=qi: nc.vector.tensor_tensor(out=fs[:, 2 * qi + 1:2 * qi + 2], in0=fs[:, 2 * qi + 1:2 * qi + 2],
                                                                              in1=neglam[:], op=ALU.mult),
                                   reads=[fs_t, small_t], writes=[fs_t])
                        for qi in range(2):
                            si = sets[qi]
                            trk.op(DVE, lambda qi=qi, si=si: nc.vector.tensor_scalar(out=dbuf[si][:], in0=ob[:, 2 * qi, 0:256],
                                                                                     scalar1=fs[:, 2 * qi:2 * qi + 1], scalar2=None,
                                                                                     op0=ALU.mult),
                                   reads=[ob_t[2 * qi], fs_t], writes=[dbuf_t[si]])
                            trk.op(DVE, lambda qi=qi, si=si: nc.vector.scalar_tensor_tensor(
                                out=dbuf[si][:], in0=ob[:, 2 * qi + 1, 0:256], scalar=fs[:, 2 * qi + 1:2 * qi + 2], in1=dbuf[si][:],
                                op0=ALU.mult, op1=ALU.add),
                                reads=[ob_t[2 * qi + 1], fs_t, dbuf_t[si]], writes=[dbuf_t[si]])

                    def st_ss():
                        for qi in range(2):
                            si = sets[qi]
                            trk.op(DVE, lambda qi=qi, si=si: nc.vector.scalar_tensor_tensor(
                                out=ob[:, 2 * qi, 0:256], in0=dbuf[si][:], scalar=1.0, in1=dbuf[si][:],
                                op0=ALU.mult, op1=ALU.mult, accum_out=fs[:, 4 + qi:5 + qi]),
                                reads=[dbuf_t[si]], writes=[ob_t[2 * qi], fs_t])
                        trk.op(DVE, lambda: nc.vector.tensor_scalar(out=fs[:, 4:6], in0=fs[:, 4:6], scalar1=1.0 / 256, scalar2=EPS,
                                                                    op0=ALU.mult, op1=ALU.add), reads=[fs_t], writes=[fs_t])
                        trk.op(POOL, lambda: nc.gpsimd.tensor_tensor(out=fs[:, 4:6], in0=fs[:, 4:6], in1=mhalf[:, 0:2], op=ALU.pow),
                               reads=[fs_t, consts_t], writes=[fs_t])

                    def st_y():
                        for qi in range(2):
                            si = sets[qi]
                            tt = 2 * qb + qi
                            trk.op(DVE, lambda qi=qi, si=si, tt=tt: nc.vector.scalar_tensor_tensor(
                                out=yb[si][:], in0=dbuf[si][:], scalar=fs[:, 4 + qi:5 + qi], in1=zs[:, tt, :],
                                op0=ALU.mult, op1=ALU.mult),
                                reads=[dbuf_t[si], fs_t, zs_t], writes=[yb_t[si]])

                    def st_tr():
                        for qi in range(2):
                            si = sets[qi]
                            tt = 2 * qb + qi
                            b = nbank()
                            pv = banks[b][:].bitcast(BF16)[:, 0:256].rearrange("p (c t) -> p c t", c=2)
                            for c in range(2):
                                trk.op(PE, lambda c=c, b=b, pv=pv, si=si: nc.tensor.transpose(
                                    out=pv[:, c, :], in_=yb[si][:, c * 128:(c + 1) * 128], identity=ident[:]),
                                    reads=[yb_t[si], consts_t], writes=[bank_t[b]], mark=(c == 1))
                            copy_on(DVE, yTf[:, :, tt * 128:(tt + 1) * 128], pv, [bank_t[b]], [yT_t[tt]])
                            yt_ready.add((seq, hd_, tt))

                    st_evac()
                    later(1, st_d)
                    later(3, st_ss)
                    later(8, st_y)
                    later(12, st_tr)

                def make_wo(tt, wo, ko, last_head):
                    def f():
                        for dc in range(2):
                            b = nbank()
                            for c in range(2):
                                trk.op(PE, lambda c=c, dc=dc, b=b: nc.tensor.matmul(
                                    out=banks[b][:], lhsT=yTf[:, c, tt * 128:(tt + 1) * 128], rhs=wo[:, c, dc * 512:(dc + 1) * 512],
                                    start=(c == 0), stop=(c == 1)),
                                    reads=[yT_t[tt], wsl_t[ko]], writes=[bank_t[b]], mark=(c == 1))
                            trk.op(DVE, lambda dc=dc, b=b: nc.vector.tensor_tensor(
                                out=h_sb[:, tt, dc * 512:(dc + 1) * 512], in0=banks[b][:],
                                in1=h_sb[:, tt, dc * 512:(dc + 1) * 512], op=ALU.add),
                                reads=[bank_t[b], h_t[tt]], writes=[h_t[tt]])
                    return f

                items = [(qb, kt) for qb in range(NQB) for kt in range(2 * qb + 2)]
                ctxs = {}
                for j in range(min(2, len(items))):
                    ctxs[j] = emit_s(*items[j])
                for i, (qb, kt) in enumerate(items):
                    if i + 2 < len(items):
                        ctxs[i + 2] = emit_s(*items[i + 2])
                    emit_pv(qb, kt, ctxs.pop(i))
                    tick()
                    if kt == 2 * qb + 1:
                        emit_fin(qb)
                pool_state["ids"] = list(range(8))
                if hd == 7:
                    flush_all()
                    _sync_all(trk, engines, SP)
                    if seq + 1 < NSEQ and do_a:
                        prefetch_a([4, 5, 6])
                for tt in range(NT):
                    wo_pending.append(((seq, hd, tt), make_wo(tt, wo, ko, hd == 7)))
                if hd == 7:
                    for tt in range(NT + 2):
                        if tt < NT:
                            n0 = len(wo_pending)
                            pop_wo(1)
                            assert len(wo_pending) == n0 - 1
                            if final_norm:
                                final_stats(tt)
                        if tt >= 2:
                            final_store(seq, tt - 2, final_norm)
                            if tt - 2 == 3 and seq + 1 < NSEQ:
                                load_x(seq + 1, 0)
                    assert not wo_pending

        setup_consts()
        _sync_all(trk, engines, SP)
        prepass()
        _sync_all(trk, engines, SP)
        for seq in range(NSEQ):
            if seq == 0 or not do_b:
                load_x(seq, 0)
            if do_a:
                for blk in range(NBLK):
                    layer_a_block(blk, seq)
                _sync_all(trk, engines, SP)
            else:
                for blk in range(1, NBLK):
                    load_x(seq, blk)
            if do_b:
                layer_b(seq)
            else:
                for tt in range(NT):
                    final_store(seq, tt, False)
                _sync_all(trk, engines, SP)
        for i in range(4):
            if st_sem[i].count:
                SP.wait((st_sem[i], st_sem[i].count))
                POOL.wait((st_sem[i], st_sem[i].count))
    return nc


def _sync_all(trk, engines, SP):
    for Ej in engines:
        if Ej.pending:
            raise RuntimeError("pending unmarked instruction at barrier on " + Ej.name)
    for Ei in engines + [SP]:
        for Ej in engines:
            if Ej is Ei or Ej.semw.count == 0:
                continue
            Ei.wait((Ej.semw, Ej.semw.count))


_NC_CACHE = {}


def _get_nc(key, **kw):
    if key not in _NC_CACHE:
        _NC_CACHE[key] = build(**kw)
    return _NC_CACHE[key]


def _param_map(inputs):
    f = lambda a: np.ascontiguousarray(np.asarray(a, dtype=np.float32))
    m = {
        "a_norm_g": f(inputs["a_norm_g"]).reshape(D),
        "a_w_in": f(inputs["a_w_in"]).reshape(D, 3 * E),
        "a_ln_g": f(inputs["a_ln_g"]).reshape(E),
        "a_ln_b": f(inputs["a_ln_b"]).reshape(E),
        "a_w_s": f(inputs["a_w_s"]).reshape(16, 128, 128),
        "a_b_s": f(inputs["a_b_s"]).reshape(16 * 128),
        "a_w_out": f(inputs["a_w_out"]).reshape(E, D),
        "b_norm_g": f(inputs["b_norm_g"]).reshape(D),
        "b_w_qz": f(inputs["b_w_qz"]).reshape(D, 2 * E),
        "b_lam_q1": f(inputs["b_lam_q1"]).reshape(128),
        "b_lam_k1": f(inputs["b_lam_k1"]).reshape(128),
        "b_lam_q2": f(inputs["b_lam_q2"]).reshape(128),
        "b_lam_k2": f(inputs["b_lam_k2"]).reshape(128),
        "b_subln_g": f(inputs["b_subln_g"]).reshape(256),
        "b_w_o": f(inputs["b_w_o"]).reshape(E, D),
        "kv_norm_g": f(inputs["kv_norm_g"]).reshape(D),
        "w_kv": f(inputs["w_kv"]).reshape(D, 2 * E),
        "final_g": f(inputs["final_g"]).reshape(D),
    }
    return m


def kernel(**inputs):
    x = np.ascontiguousarray(np.asarray(inputs["x"], dtype=np.float32))
    B, S, _ = x.shape
    per = B // NCORES
    nc = _get_nc(("full", per, S), NSEQ=per, SEQ=S)
    pm = _param_map(inputs)
    in_maps = []
    for c in range(NCORES):
        m = dict(pm)
        m["x"] = x[c * per:(c + 1) * per].reshape(per * S, D)
        in_maps.append(m)
    res = run_bass_kernel_spmd(nc, in_maps, core_ids=list(range(NCORES)))
    outs = [np.asarray(r["out"]).reshape(per, S, D) for r in res.results]
    return np.concatenate(outs, axis=0).astype(np.float32)
```
